# Optimizing a Trainium2 kernel written in Bass

```python
import math
import jax
import jax.numpy as jnp
from jax import lax
import numpy as np

D_MODEL = 4096
BATCH = 1
SEQ = 8192
DEPTH = 2
DEC_BATCH = 16
DEC_SEQ = 64
PAST_LEN = 2048

CHUNK = 64
Q_BLOCK = 128
HEAD_DIM = 128
H_A = D_MODEL // (2 * HEAD_DIM)
KV_A = H_A // 4
H_IDX = 16
D_IDX = 64
TOPK_MAX = 256
H_B = D_MODEL // (4 * HEAD_DIM)
H_C = D_MODEL // (2 * HEAD_DIM)
H_D = D_MODEL // (2 * HEAD_DIM)
BAND_CHUNKS = 8
REL_CLIP = 128
T5_BUCKETS = 32
T5_MAX_DIST = 128
PL_DIM = 256
RMS_EPS = 1e-6
N_EVEN = (DEPTH + 1) // 2
N_ODD = DEPTH // 2
EVEN_SIZES = (H_A * HEAD_DIM, KV_A * HEAD_DIM, KV_A * HEAD_DIM, H_IDX * D_IDX, D_IDX, H_IDX, H_A * HEAD_DIM,
              H_B * 2 * HEAD_DIM, H_B * 2 * HEAD_DIM, H_B * 2 * HEAD_DIM, H_B * 2 * HEAD_DIM)
ODD_SIZES = (H_C * HEAD_DIM,) * 4 + (H_D * HEAD_DIM,) * 4
MIX_EVEN = H_A * HEAD_DIM + H_B * 2 * HEAD_DIM
MIX_ODD = (H_C + H_D) * HEAD_DIM

kernel_name = "hybrid_chunk_stream_encoder_step"


def rms_norm(x, g):
    xf = x.astype(jnp.float32)
    y = xf * lax.rsqrt(jnp.mean(xf * xf, axis=-1, keepdims=True) + RMS_EPS)
    return (y * g.astype(jnp.float32)).astype(x.dtype)


def split_cols(y, sizes):
    return jnp.split(y, np.cumsum(sizes)[:-1].tolist(), axis=-1)


def t5_bucket(rel):
    half = T5_BUCKETS // 2
    max_exact = half // 2
    n = jnp.abs(rel)
    nf = jnp.maximum(n, 1).astype(jnp.float32)
    large = max_exact + (jnp.log(nf / max_exact) / math.log(T5_MAX_DIST / max_exact)
                         * (half - max_exact)).astype(jnp.int32)
    large = jnp.minimum(large, half - 1)
    return jnp.where(rel < 0, half, 0) + jnp.where(n < max_exact, n, large)


def chunk_visible(qpos, kpos):
    return (kpos // CHUNK) <= (qpos // CHUNK)


def query_blocks(fn, *xs):
    n = xs[0].shape[1]
    if n <= Q_BLOCK:
        return fn(*xs)
    nb = n // Q_BLOCK
    blocks = tuple(x.reshape(x.shape[0], nb, Q_BLOCK, *x.shape[2:]).swapaxes(0, 1) for x in xs)
    out = lax.map(lambda b: fn(*b), blocks)
    return out.swapaxes(0, 1).reshape(out.shape[1], n, *out.shape[3:])


def dsa_core(q, qi, wi, qpos, k, v, ki, kpos, t5_tab, n_sel):
    B, Tq = q.shape[:2]
    s = jnp.einsum("bqhd,bsd->bqhs", qi.astype(jnp.float32), ki.astype(jnp.float32)) * (D_IDX ** -0.5)
    score = jnp.einsum("bqhs,bqh->bqs", jax.nn.relu(s), wi.astype(jnp.float32)) * (H_IDX ** -0.5)
    score = jnp.where(chunk_visible(qpos[:, None], kpos[None, :]), score, -jnp.inf)
    _, idx = lax.top_k(score, n_sel)
    sel_pos = kpos[idx]
    valid = chunk_visible(qpos[None, :, None], sel_pos)
    gather = jax.vmap(lambda a, i: a[i])
    ks = gather(k, idx)
    vs = gather(v, idx)
    G = H_A // KV_A
    qg = q.reshape(B, Tq, KV_A, G, HEAD_DIM)
    logits = jnp.einsum("bqngd,bqknd->bqngk", qg, ks).astype(jnp.float32) * (HEAD_DIM ** -0.5)
    bias = t5_tab[:, :H_A][t5_bucket(qpos[None, :, None] - sel_pos)]
    bias = jnp.moveaxis(bias, -1, 2).reshape(B, Tq, KV_A, G, n_sel)
    logits = jnp.where(valid[:, :, None, None, :], logits + bias, -jnp.inf)
    p = jax.nn.softmax(logits, axis=-1).astype(v.dtype)
    o = jnp.einsum("bqngk,bqknd->bqngd", p, vs)
    return o.reshape(B, Tq, H_A * HEAD_DIM)


def diff_core(q, qpos, k, v, kpos, t5_tab, lam, lam_init, subln_g):
    B, Tq = q.shape[:2]
    logits = jnp.einsum("bqhcd,bshcd->bhcqs", q, k).astype(jnp.float32) * (HEAD_DIM ** -0.5)
    bias = t5_tab[:, H_A:][t5_bucket(qpos[:, None] - kpos[None, :])]
    bias = jnp.transpose(bias, (2, 0, 1))[None, :, None]
    vis = chunk_visible(qpos[:, None], kpos[None, :])
    a = jax.nn.softmax(jnp.where(vis, logits + bias, -jnp.inf), axis=-1)
    attn = (a[:, :, 0] - lam * a[:, :, 1]).astype(v.dtype)
    o = jnp.einsum("bhqs,bshe->bqhe", attn, v)
    o = rms_norm(o, subln_g) * (1.0 - lam_init)
    return o.reshape(B, Tq, H_B * 2 * HEAD_DIM)


def stick_core(q, qpos, k, v, kpos):
    B, Tq = q.shape[:2]
    z = jnp.einsum("bqhd,bshd->bhqs", q, k).astype(jnp.float32) * (HEAD_DIM ** -0.5)
    before = kpos[None, :] < qpos[:, None]
    log_fail = jnp.where(before, jax.nn.log_sigmoid(-z), 0.0)
    later = lax.cumsum(log_fail, axis=3, reverse=True) - log_fail
    w = jnp.where(before, jnp.exp(jax.nn.log_sigmoid(z) + later), 0.0).astype(v.dtype)
    o = jnp.einsum("bhqs,bshd->bqhd", w, v)
    return o.reshape(B, Tq, H_C * HEAD_DIM)


def band_core(q, qpos, k, v, kpos, rel_tab):
    B, Tq = q.shape[:2]
    logits = jnp.einsum("bqhd,bshd->bhqs", q, k).astype(jnp.float32) * (HEAD_DIM ** -0.5)
    rel = jnp.clip(qpos[:, None] - kpos[None, :], -REL_CLIP, REL_CLIP) + REL_CLIP
    bias = jnp.transpose(rel_tab[rel], (2, 0, 1))
    qc = qpos[:, None] // CHUNK
    kc = kpos[None, :] // CHUNK
    vis = (kpos[None, :] >= 0) & (kc <= qc) & (kc >= qc - BAND_CHUNKS)
    p = jax.nn.softmax(jnp.where(vis, logits + bias, -jnp.inf), axis=-1).astype(v.dtype)
    o = jnp.einsum("bhqs,bshd->bqhd", p, v)
    return o.reshape(B, Tq, H_D * HEAD_DIM)


def band_prompt(q, k, v, rel_tab):
    B, T = q.shape[:2]
    pad = BAND_CHUNKS * CHUNK
    band = (BAND_CHUNKS + 1) * CHUNK
    kp = jnp.pad(k, ((0, 0), (pad, 0), (0, 0), (0, 0)))
    vp = jnp.pad(v, ((0, 0), (pad, 0), (0, 0), (0, 0)))

    def one_chunk(c):
        start = c * CHUNK
        qc = lax.dynamic_slice_in_dim(q, start, CHUNK, axis=1)
        kc = lax.dynamic_slice_in_dim(kp, start, band, axis=1)
        vc = lax.dynamic_slice_in_dim(vp, start, band, axis=1)
        qpos = start + jnp.arange(CHUNK)
        kpos = start - pad + jnp.arange(band)
        return band_core(qc, qpos, kc, vc, kpos, rel_tab)

    out = lax.map(one_chunk, jnp.arange(T // CHUNK))
    return out.swapaxes(0, 1).reshape(B, T, out.shape[-1])


def even_layer(h, past, w_in, w_out, t5_tab, lam_vec, subln_g, lam_init):
    B, T, _ = h.shape
    aq, ak, av, aqi, aki, aw, ag, bq, bk, bv, bg = split_cols(h @ w_in, EVEN_SIZES)
    aq = aq.reshape(B, T, H_A, HEAD_DIM)
    aqi = aqi.reshape(B, T, H_IDX, D_IDX)
    bq = bq.reshape(B, T, H_B, 2, HEAD_DIM)
    new_a = jnp.stack([ak.reshape(B, T, KV_A, HEAD_DIM), av.reshape(B, T, KV_A, HEAD_DIM)], axis=2)
    new_b = jnp.stack([bk.reshape(B, T, H_B, 2 * HEAD_DIM), bv.reshape(B, T, H_B, 2 * HEAD_DIM)], axis=2)
    if past is None:
        p0 = 0
        full_a, full_ki, full_b = new_a, aki, new_b
    else:
        past_a, past_ki, past_b = past
        p0 = past_a.shape[1]
        full_a = jnp.concatenate([past_a, new_a], axis=1)
        full_ki = jnp.concatenate([past_ki, aki], axis=1)
        full_b = jnp.concatenate([past_b, new_b], axis=1)
    L = p0 + T
    kpos = jnp.arange(L)
    qpos = p0 + jnp.arange(T)
    n_sel = min(TOPK_MAX, L // 4)
    ka, va = full_a[:, :, 0], full_a[:, :, 1]
    kb = full_b[:, :, 0].reshape(B, L, H_B, 2, HEAD_DIM)
    vb = full_b[:, :, 1]
    lq1, lk1, lq2, lk2 = lam_vec.astype(jnp.float32)
    lam = jnp.exp(jnp.sum(lq1 * lk1)) - jnp.exp(jnp.sum(lq2 * lk2)) + lam_init
    o_a = query_blocks(lambda q, qi, wi, qp: dsa_core(q, qi, wi, qp[0], ka, va, full_ki, kpos, t5_tab, n_sel),
                       aq, aqi, aw, qpos[None])
    o_b = query_blocks(lambda q, qp: diff_core(q, qp[0], kb, vb, kpos, t5_tab, lam, lam_init, subln_g),
                       bq, qpos[None])
    mixed = jnp.concatenate([o_a * jax.nn.silu(ag), o_b * jax.nn.silu(bg)], axis=-1)
    return mixed @ w_out, new_a, aki, new_b


def odd_layer(h, past, w_in, w_out, rel_tab):
    B, T, _ = h.shape
    cq, ck, cv, cg, dq, dk, dv, dg = split_cols(h @ w_in, ODD_SIZES)
    cq, ck, cv = (t.reshape(B, T, H_C, HEAD_DIM) for t in (cq, ck, cv))
    dq, dk, dv = (t.reshape(B, T, H_D, HEAD_DIM) for t in (dq, dk, dv))
    new_c = jnp.stack([ck, cv], axis=2)
    new_d = jnp.stack([dk, dv], axis=2)
    if past is None:
        kpos = jnp.arange(T)
        o_c = query_blocks(lambda q, qp: stick_core(q, qp[0], ck, cv, kpos), cq, kpos[None])
        o_d = band_prompt(dq, dk, dv, rel_tab)
        d_state = new_d[:, T - min(BAND_CHUNKS * CHUNK, T):]
    else:
        past_c, past_d = past
        p0 = past_c.shape[1]
        win = past_d.shape[1]
        full_c = jnp.concatenate([past_c, new_c], axis=1)
        kpos = jnp.arange(p0 + T)
        qpos = p0 + jnp.arange(T)
        kc, vc = full_c[:, :, 0], full_c[:, :, 1]
        o_c = query_blocks(lambda q, qp: stick_core(q, qp[0], kc, vc, kpos), cq, qpos[None])
        full_d = jnp.concatenate([past_d, new_d], axis=1)
        dpos = p0 - win + jnp.arange(win + T)
        o_d = band_core(dq, qpos, full_d[:, :, 0], full_d[:, :, 1], dpos, rel_tab)
        d_state = full_d[:, T:]
    mixed = jnp.concatenate([o_c * jax.nn.silu(cg), o_d * jax.nn.silu(dg)], axis=-1)
    return mixed @ w_out, new_c, d_state


def finish_layer(x, mix, p_i, g_post, w_proj, g_pl, w_gate):
    x = x + rms_norm(mix, g_post)
    e = rms_norm(p_i @ w_proj, g_pl)
    return x + e * jax.nn.sigmoid(x @ w_gate)


def setup_inputs(seed: int = 0) -> dict:
    key = jax.random.key(seed)
    ks = jax.random.split(key, 22)

    def nrm(k, shape, scale=1.0):
        return scale * jax.random.normal(k, shape, jnp.float32)

    d_win = min(BAND_CHUNKS * CHUNK, PAST_LEN)
    in_even = int(sum(EVEN_SIZES))
    in_odd = int(sum(ODD_SIZES))
    return {
        "x_prompt": nrm(ks[0], (BATCH, SEQ, D_MODEL)),
        "x_sample": nrm(ks[1], (DEC_BATCH, DEC_SEQ, D_MODEL)),
        "p_prompt": nrm(ks[2], (DEPTH, BATCH, SEQ, PL_DIM)),
        "p_sample": nrm(ks[3], (DEPTH, DEC_BATCH, DEC_SEQ, PL_DIM)),
        "cache_a_kv": nrm(ks[4], (N_EVEN, DEC_BATCH, PAST_LEN, 2, KV_A, HEAD_DIM)),
        "cache_a_kidx": nrm(ks[5], (N_EVEN, DEC_BATCH, PAST_LEN, D_IDX)),
        "cache_b_kv": nrm(ks[6], (N_EVEN, DEC_BATCH, PAST_LEN, 2, H_B, 2 * HEAD_DIM)),
        "cache_c_kv": nrm(ks[7], (N_ODD, DEC_BATCH, PAST_LEN, 2, H_C, HEAD_DIM)),
        "cache_d_kv": nrm(ks[8], (N_ODD, DEC_BATCH, d_win, 2, H_D, HEAD_DIM)),
        "norm_pre": 1.0 + nrm(ks[9], (DEPTH, D_MODEL), 0.1),
        "norm_post": 1.0 + nrm(ks[10], (DEPTH, D_MODEL), 0.1),
        "w_in_even": nrm(ks[11], (N_EVEN, D_MODEL, in_even), D_MODEL ** -0.5),
        "w_out_even": nrm(ks[12], (N_EVEN, MIX_EVEN, D_MODEL), MIX_EVEN ** -0.5),
        "t5_bias": nrm(ks[13], (T5_BUCKETS, H_A + H_B), 0.2),
        "diff_lambda": nrm(ks[14], (N_EVEN, 4, HEAD_DIM), 0.1),
        "diff_subln": 1.0 + nrm(ks[15], (N_EVEN, 2 * HEAD_DIM), 0.1),
        "w_in_odd": nrm(ks[16], (N_ODD, D_MODEL, in_odd), D_MODEL ** -0.5),
        "w_out_odd": nrm(ks[17], (N_ODD, MIX_ODD, D_MODEL), MIX_ODD ** -0.5),
        "d_rel_bias": nrm(ks[18], (N_ODD, 2 * REL_CLIP + 1, H_D), 0.2),
        "w_pl_proj": nrm(ks[19], (DEPTH, PL_DIM, D_MODEL), PL_DIM ** -0.5),
        "pl_norm": 1.0 + nrm(ks[20], (DEPTH, D_MODEL), 0.1),
        "w_pl_gate": nrm(ks[21], (DEPTH, D_MODEL, D_MODEL), D_MODEL ** -0.5),
    }


def reference(x_prompt, x_sample, p_prompt, p_sample, cache_a_kv, cache_a_kidx, cache_b_kv, cache_c_kv,
              cache_d_kv, norm_pre, norm_post, w_in_even, w_out_even, t5_bias, diff_lambda, diff_subln,
              w_in_odd, w_out_odd, d_rel_bias, w_pl_proj, pl_norm, w_pl_gate):
    xp, xs = x_prompt, x_sample
    a_kv_p, a_kv_s, a_ki_p, a_ki_s, b_kv_p, b_kv_s = [], [], [], [], [], []
    c_kv_p, c_kv_s, d_kv_p, d_kv_s = [], [], [], []
    for i in range(DEPTH):
        j = i // 2
        hp = rms_norm(xp, norm_pre[i])
        hs = rms_norm(xs, norm_pre[i])
        if i % 2 == 0:
            lam_init = 0.8 - 0.6 * math.exp(-0.3 * i)
            prm = (w_in_even[j], w_out_even[j], t5_bias, diff_lambda[j], diff_subln[j], lam_init)
            mp, akv, aki, bkv = even_layer(hp, None, *prm)
            a_kv_p.append(akv)
            a_ki_p.append(aki)
            b_kv_p.append(bkv)
            ms, akv, aki, bkv = even_layer(hs, (cache_a_kv[j], cache_a_kidx[j], cache_b_kv[j]), *prm)
            a_kv_s.append(akv)
            a_ki_s.append(aki)
            b_kv_s.append(bkv)
        else:
            prm = (w_in_odd[j], w_out_odd[j], d_rel_bias[j])
            mp, ckv, dkv = odd_layer(hp, None, *prm)
            c_kv_p.append(ckv)
            d_kv_p.append(dkv)
            ms, ckv, dkv = odd_layer(hs, (cache_c_kv[j], cache_d_kv[j]), *prm)
            c_kv_s.append(ckv)
            d_kv_s.append(dkv)
        xp = finish_layer(xp, mp, p_prompt[i], norm_post[i], w_pl_proj[i], pl_norm[i], w_pl_gate[i])
        xs = finish_layer(xs, ms, p_sample[i], norm_post[i], w_pl_proj[i], pl_norm[i], w_pl_gate[i])
    return (xp, xs, jnp.stack(a_kv_p), jnp.stack(a_kv_s), jnp.stack(a_ki_p), jnp.stack(a_ki_s),
            jnp.stack(b_kv_p), jnp.stack(b_kv_s), jnp.stack(c_kv_p), jnp.stack(c_kv_s),
            jnp.stack(d_kv_p), jnp.stack(d_kv_s))
```

```python
import math
import os
import numpy as np
import ml_dtypes
import concourse.bass as bass
import concourse.mybir as mybir
from concourse.bass_utils import run_bass_kernel_spmd

F32 = mybir.dt.float32
BF16 = mybir.dt.bfloat16
ALU = mybir.AluOpType
AF = mybir.ActivationFunctionType
AX = mybir.AxisListType

NCORES = 8
D = 4096
KC = 32
NT = 9
NTOK = NT * 128
NEG = -30000.0
EPS = 1e-6
SCALE = 128 ** -0.5
STAGE = int(os.environ.get("MK_STAGE", "99"))
DEBUG = int(os.environ.get("MK_DEBUG", "0"))
STQ = os.environ.get("MK_STQ", "sync")
BIG = ("w_in_even", "w_out_even", "w_in_odd", "w_out_odd", "w_pl_gate0", "w_pl_gate1", "cache_a_kv", "cache_a_kidx", "cache_b_kv", "cache_c_kv", "cache_d_kv")
DBGOUT = int(os.environ.get("MK_DBGOUT", "0"))
USED = ("x", "gpre", "ident", "w_in_even", "cache_a_kv", "cache_a_kidx", "cache_b_kv", "tbA", "tbB", "maskP", "tbAs",
        "tbBs", "maskS", "constAB", "lamv", "subln", "w_out_even", "w_pl_gate0", "w_pl_gate1", "w_pl_proj", "gpost", "gpl", "p",
        "w_in_odd", "w_out_odd", "cache_c_kv", "cache_d_kv", "maskC", "maskCs", "tbD", "maskD", "tbDs", "maskDs")


class Sem:
    def __init__(self, nc, name):
        self.h = nc.alloc_semaphore(name=name)
        self.count = 0


class Res:
    __slots__ = ("name", "w", "rs", "dsem", "excl", "persist")

    def __init__(self, name="r", excl=False, persist=False):
        self.name = name
        self.excl = excl
        self.persist = persist
        self.w = None
        self.rs = []
        self.dsem = None


class Prog:
    ENGS = ("sync", "act", "pool", "pe", "dve")

    def __init__(self, nc):
        self.nc = nc
        self.ops = {e: [] for e in self.ENGS}
        self.esem = {e: Sem(nc, "s_" + e) for e in self.ENGS}
        self.known = {e: {} for e in self.ENGS}
        self.allsems = list(self.esem.values())
        self.n_ops = 0
        self.free_sems = []
        self.phase_sems = []
        self.cache = {}
        self.sb_off = 16384

    def sb(self, name, shape, dt):
        nbytes = int(np.prod(shape[1:])) * (4 if dt == F32 else 2)
        off = (self.sb_off + 63) // 64 * 64
        assert off + nbytes <= 224 * 1024, ("SBUF overflow", name, off, nbytes)
        self.sb_off = off + nbytes
        self.n_ops += 1
        return self.nc.alloc_sbuf_tensor_at("%s_%d" % (name, self.n_ops), list(shape), dt, offset=off)

    def res(self, name="r", excl=False, persist=False):
        return Res(name, excl, persist)

    def dsem_for(self, r):
        if r.dsem is None:
            if not r.persist and self.free_sems:
                r.dsem = self.free_sems.pop()
            else:
                r.dsem = Sem(self.nc, "d%d" % len(self.allsems))
                self.allsems.append(r.dsem)
            if not r.persist:
                self.phase_sems.append(r.dsem)
        return r.dsem

    def _issue(self, eng, fn, reads, writes, sem, inc, extra_deps=()):
        ex = [r for r in reads if r.excl]
        if ex:
            reads = [r for r in reads if not r.excl]
            writes = list(writes) + ex
        deps = {}
        own = self.esem[eng] if eng == "pe" else None

        def add(ev):
            if ev is None:
                return
            s, v = ev
            if s is own:
                return
            if deps.get(s, 0) < v:
                deps[s] = v

        for r in reads:
            add(r.w)
        for r in writes:
            add(r.w)
            for ev in r.rs:
                add(ev)
        for ev in extra_deps:
            add(ev)
        kn = self.known[eng]
        waits = []
        for s, v in deps.items():
            if kn.get(s, 0) < v:
                kn[s] = v
                waits.append((s.h, v))
        sem.count += inc
        ev = (sem, sem.count)
        for r in reads:
            r.rs.append(ev)
            if len(r.rs) > 64:
                mx = {}
                for s, v in r.rs:
                    if mx.get(s, 0) < v:
                        mx[s] = v
                r.rs = list(mx.items())
        for r in writes:
            r.w = ev
            r.rs = []
        semh = sem.h
        self.n_ops += 1

        def emit(h):
            for sh, v in waits:
                h.wait_ge(sh, v)
            fn(h).then_inc(semh, inc)

        self.ops[eng].append(emit)
        return ev

    def op(self, eng, fn, reads=(), writes=()):
        return self._issue(eng, fn, reads, writes, self.esem[eng], 1)

    def dma(self, eng, fn, reads=(), writes=(), slot=None):
        sem = self.dsem_for(slot)
        extra = [(sem, sem.count)] if sem.count > 0 else []
        return self._issue(eng, fn, reads, writes, sem, 16, extra)

    def coll(self, fn, reads=(), writes=(), slot=None):
        sem = self.dsem_for(slot)
        extra = [(sem, sem.count)] if sem.count > 0 else []
        return self._issue("pool", fn, reads, writes, sem, 1, extra)

    def barrier(self, exclude=()):
        evs = [(s, s.count) for s in self.allsems if s.count > 0 and s not in exclude]
        for eng in self.ENGS:
            kn = self.known[eng]
            waits = []
            for s, v in evs:
                if kn.get(s, 0) < v:
                    kn[s] = v
                    waits.append((s.h, v))

            def emit(h, waits=waits):
                for sh, v in waits:
                    h.wait_ge(sh, v)

            self.ops[eng].append(emit)
        self.sb_off = self.sb_keep
        self.cache = {}
        self.free_sems.extend(self.phase_sems)
        self.phase_sems = []

    def finish(self):
        self.barrier()
        nc = self.nc
        ops = self.ops
        with nc.Block() as block:
            @block.sync
            def _(e):
                for f in ops["sync"]:
                    f(e)

            @block.scalar
            def _(e):
                for f in ops["act"]:
                    f(e)

            @block.gpsimd
            def _(e):
                for f in ops["pool"]:
                    f(e)

            @block.tensor
            def _(e):
                for f in ops["pe"]:
                    f(e)

            @block.vector
            def _(e):
                for f in ops["dve"]:
                    f(e)


class Rot:
    def __init__(self, P, name, n, shape, dt):
        self.t = [P.sb(name + str(i), shape, dt) for i in range(n)]
        self.r = [P.res(name + str(i)) for i in range(n)]
        self.i = 0

    def next(self):
        k = self.i % len(self.t)
        self.i += 1
        return self.t[k], self.r[k]


def _t5_bucket(rel):
    half, max_exact = 16, 8
    n = np.abs(rel)
    nf = np.maximum(n, 1).astype(np.float32)
    large = max_exact + (np.log(nf / np.float32(max_exact)) / np.float32(math.log(128 / max_exact))
                         * np.float32(half - max_exact)).astype(np.int32)
    large = np.minimum(large, half - 1)
    return np.where(rel < 0, half, 0) + np.where(n < max_exact, n, large)


def build_program():
    nc = bass.Bass("TRN2", target_bir_lowering=False)
    P = Prog(nc)

    def din(name, shape, dt=F32):
        if name not in USED or (DEBUG and name in BIG):
            return nc.dram_tensor(name, list(shape), dt).ap()
        return nc.dram_tensor(name, list(shape), dt, kind="ExternalInput").ap()

    def dout(name, shape, dt=F32):
        return nc.dram_tensor(name, list(shape), dt, kind="ExternalOutput").ap()

    def dscr(name, shape, dt=BF16):
        return nc.dram_tensor(name, list(shape), dt).ap()

    x_in = din("x", [NTOK, D])
    p_in = din("p", [2, NTOK, 256])
    w_dbg = nc.dram_tensor("w_dbg", [D, 512], F32, kind="ExternalInput").ap() if DEBUG else None
    W_RES = {}
    W_PENDING = []

    def din_w(name, ncols, key=None):
        sh_in = din(name, [D // NCORES, ncols])
        sh = nc.dram_tensor(name + "_sh", [D // NCORES, ncols], F32).ap()
        full = nc.dram_tensor(name + "_full", [D, ncols], F32).ap()
        W_PENDING.append((name, sh_in, sh, full))
        return full

    w_in_e = din_w("w_in_even", 14416)
    w_out_e = din_w("w_out_even", D)
    w_in_o = din_w("w_in_odd", 16384)
    w_out_o = din_w("w_out_odd", D)
    w_plp = din("w_pl_proj", [2, 256, D])
    w_plg = [din_w("w_pl_gate0", D), din_w("w_pl_gate1", D)]
    gpre_in = din("gpre", [2, 128, KC])
    gpost_in = din("gpost", [2, 128, D])
    gpl_in = din("gpl", [2, 128, D])
    ident_in = din("ident", [128, 128])
    ca_kv = din("cache_a_kv", [2, 2048, 1024])
    ca_ki = din("cache_a_kidx", [2, 2048, 64])
    cb_kv = din("cache_b_kv", [2, 2048, 4096])
    cc_kv = din("cache_c_kv", [2, 2048, 4096])
    cd_kv = din("cache_d_kv", [2, 512, 4096])

    y_out = dout("y", [NTOK, D])
    o_akv = dout("o_akv", [NTOK, 1024])
    o_aki = dout("o_aki", [NTOK, 64])
    o_bkv = dout("o_bkv", [NTOK, 4096])
    o_ckv = dout("o_ckv", [NTOK, 4096])
    o_dkv = dout("o_dkv", [NTOK, 4096])
    o_ds = dout("o_ds", [2, 512, 4096])
    R_out = P.res("outputs", persist=True)

    QA = dscr("QA", [NTOK, 2048]); QI = dscr("QI", [NTOK, 1024]); AW = dscr("AW", [NTOK, 16], F32)
    GA = dscr("GA", [NTOK, 2048]); QB = dscr("QB", [NTOK, 2048]); GB = dscr("GB", [NTOK, 2048])
    KVA_in = dscr("KVA_in", [1024, 1024]); KI_in = dscr("KI_in", [1024, 64]); KVB_in = dscr("KVB_in", [1024, 4096])
    KVA_all = dscr("KVA_all", [8 * 1024, 1024]); KI_all = dscr("KI_all", [8 * 1024, 64])
    KVB_all = dscr("KVB_all", [8 * 1024, 4096])
    KVS_A = dscr("KVS_A", [2, 2176, 1024]); KIS = dscr("KIS", [2, 2176, 64]); KVS_B = dscr("KVS_B", [2, 2176, 4096])
    R_scr = {k: P.res(k, persist=True) for k in ("QA", "QI", "AW", "GA", "QB", "GB", "KVA_in", "KI_in", "KVB_in", "KVA_all",
                                   "KI_all", "KVB_all", "KVS_A", "KIS", "KVS_B", "X1", "MIX", "MO", "SG", "SELB")}

    ident_f = P.sb("ident_f", [128, 128], F32)
    ident_b = P.sb("ident_b", [128, 128], BF16)
    gpre = P.sb("gpre", [128, 2, KC], F32)
    zeros_b = P.sb("zeros_b", [128, 4096], BF16)
    R_c = P.res("consts", persist=True)
    const_keep = P.sb_off
    P.sb_keep = const_keep

    FB = [nc.alloc_psum_tensor("fb%d" % i, [128, 512], F32) for i in range(6)]
    BB = [nc.alloc_psum_tensor("bb%d" % i, [128, 1024], BF16) for i in range(2)]
    R_fb = [P.res("fb%d" % i, excl=True) for i in range(6)]
    R_bb = [P.res("bb%d" % i, excl=True) for i in range(2)]
    rot = {"fb": 0, "bb": 0}

    def next_fb(lo=0, hi=6):
        k = lo + rot["fb"] % (hi - lo)
        rot["fb"] += 1
        return FB[k], R_fb[k]

    def next_bb():
        k = rot["bb"] % 2
        rot["bb"] += 1
        return BB[k], R_bb[k]

    P.dma("sync", lambda e: e.dma_start(out=ident_f[:], in_=ident_in), writes=[R_c], slot=R_c)
    P.dma("sync", lambda e: e.dma_start(out=gpre[:], in_=gpre_in.rearrange("l p k -> p l k")), writes=[R_c], slot=R_c)
    P.op("dve", lambda e: e.tensor_copy(out=ident_b[:], in_=ident_f[:]), reads=[R_c], writes=[R_c])
    P.op("pool", lambda e: e.memset(zeros_b[:], 0.0), writes=[R_c])
    if not DEBUG:
        wbounce = Rot(P, "wbounce", 3, [128, 4096], F32)
        for name, sh_in, sh, full in W_PENDING:
            r_sh = P.res(name + "_sh", persist=True)
            r_full = P.res(name + "_full", persist=True)
            ncw = sh.shape[-1]
            for r0 in range(0, D // NCORES, 128):
                for c0 in range(0, ncw, 4096):
                    cwid = min(4096, ncw - c0)
                    t, r = wbounce.next()
                    P.dma("sync", lambda e, t=t, sh_in=sh_in, r0=r0, c0=c0, cwid=cwid: e.dma_start(
                        out=t[:, :cwid], in_=sh_in[r0:r0 + 128, c0:c0 + cwid]), writes=[r], slot=r)
                    P.dma("sync", lambda e, t=t, sh=sh, r0=r0, c0=c0, cwid=cwid: e.dma_start(
                        out=sh[r0:r0 + 128, c0:c0 + cwid], in_=t[:, :cwid]), reads=[r], writes=[r_sh], slot=r)
            P.coll(lambda e, sh=sh, full=full: e.collective_compute("AllGather", ALU.bypass, replica_groups=[list(range(NCORES))],
                                                                    ins=[sh], outs=[full]),
                   reads=[r_sh], writes=[r_full], slot=r_full)
            W_RES[name] = r_full
        P.barrier(exclude=[r.dsem for r in W_RES.values()])

    def norm_transpose_phase(src_dram, hT, R_hT, mode, reads=()):
        xt = Rot(P, "xt", 2, [128, D], F32 if mode == "rms" else BF16)
        xb = Rot(P, "xb", 2, [128, D], BF16) if mode == "rms" else None
        st = Rot(P, "st", 2, [128, 4], F32)
        junk = P.sb("junk", [128, D], BF16)
        R_junk = P.res("junk")
        for tt in range(NT):
            t, r = xt.next()
            P.dma("sync", lambda e, t=t, tt=tt: e.dma_start(out=t[:], in_=src_dram[tt * 128:(tt + 1) * 128, :]),
                  reads=list(reads), writes=[r], slot=r)
            if mode == "rms":
                s, rs = st.next()
                b, rb = xb.next()
                P.op("act", lambda e, t=t, s=s: e.activation(out=junk[:], in_=t[:], func=AF.Square, accum_out=s[:, 0:1]),
                     reads=[r], writes=[R_junk, rs])
                P.op("dve", lambda e, s=s: e.tensor_scalar(out=s[:, 1:2], in0=s[:, 0:1], scalar1=1.0 / D, scalar2=EPS,
                                                           op0=ALU.mult, op1=ALU.add), reads=[rs], writes=[rs])
                P.op("act", lambda e, s=s: e.activation(out=s[:, 2:3], in_=s[:, 1:2], func=AF.Sqrt), reads=[rs], writes=[rs])
                P.op("dve", lambda e, s=s: e.reciprocal(out=s[:, 3:4], in_=s[:, 2:3]), reads=[rs], writes=[rs])
                P.op("dve", lambda e, t=t, s=s, b=b: e.tensor_scalar(out=b[:], in0=t[:], scalar1=s[:, 3:4], scalar2=None,
                                                                     op0=ALU.mult), reads=[r, rs], writes=[rb])
                src, rsrc = b, rb
            else:
                src, rsrc = t, r
            transpose_rows(src, rsrc, hT, R_hT, tt)

    def transpose_rows(src, rsrc, hT, R_hT, tt, nkc=KC):
        for q in range(nkc // 8):
            bb, rbb = next_bb()

            def f(e, bb=bb, q=q, src=src):
                ins = None
                for i in range(8):
                    kc = q * 8 + i
                    ins = e.transpose(out=bb[:, i * 128:(i + 1) * 128], in_=src[:, kc * 128:(kc + 1) * 128],
                                      identity=ident_b[:])
                return ins

            P.op("pe", f, reads=[rsrc, R_c], writes=[rbb])
            eng = "act" if q % 2 == 0 else "dve"
            dst = hT[:, q * 8:(q + 1) * 8, tt * 128:(tt + 1) * 128]
            srcv = bb[:].rearrange("p (k t) -> p k t", k=8)
            if eng == "act":
                P.op("act", lambda e, dst=dst, srcv=srcv: e.copy(out=dst, in_=srcv), reads=[rbb], writes=[R_hT])
            else:
                P.op("dve", lambda e, dst=dst, srcv=srcv: e.tensor_copy(out=dst, in_=srcv), reads=[rbb], writes=[R_hT])

    def gemm(hT, R_hT, W, nkc, blocks, gsel, epilogue, ntiles=NT, wname=None):
        wb = Rot(P, "wb", 2, [128, nkc, 512], BF16)
        wb_r2 = [P.res("wbx%d" % i) for i in range(2)]
        stg = Rot(P, "wstg", 3, [128, 4, 512], F32)
        ci = 0
        for bi, (c0, w) in enumerate(blocks):
            wt, rw = wb.next()
            rw2 = wb_r2[(wb.i - 1) % 2]
            for k4 in range(nkc // 4):
                s, rs = stg.next()
                cc0 = c0 if not DEBUG else 0
                Wd = W if not DEBUG else w_dbg
                srcw = Wd[k4 * 512:(k4 + 1) * 512, cc0:cc0 + w].rearrange("(k p) n -> p k n", p=128)
                P.dma("sync", lambda e, s=s, srcw=srcw, w=w: e.dma_start(out=s[:, :, :w], in_=srcw),
                      reads=([W_RES[wname]] if (wname in W_RES) else []), writes=[rs], slot=rs)
                for i in range(4):
                    kc = k4 * 4 + i
                    ci += 1
                    if gsel is not None:
                        gcol = gpre[:, gsel, kc:kc + 1]
                        if ci % 2 == 0:
                            P.op("act", lambda e, wt=wt, s=s, i=i, kc=kc, w=w, gcol=gcol: e.activation(
                                out=wt[:, kc, :w], in_=s[:, i, :w], func=AF.Copy, scale=gcol), reads=[rs, R_c], writes=[rw])
                        else:
                            P.op("dve", lambda e, wt=wt, s=s, i=i, kc=kc, w=w, gcol=gcol: e.tensor_scalar(
                                out=wt[:, kc, :w], in0=s[:, i, :w], scalar1=gcol, scalar2=None, op0=ALU.mult),
                                reads=[rs, R_c], writes=[rw2])
                    else:
                        if ci % 2 == 0:
                            P.op("act", lambda e, wt=wt, s=s, i=i, kc=kc, w=w: e.copy(out=wt[:, kc, :w], in_=s[:, i, :w]),
                                 reads=[rs], writes=[rw])
                        else:
                            P.op("dve", lambda e, wt=wt, s=s, i=i, kc=kc, w=w: e.tensor_copy(out=wt[:, kc, :w], in_=s[:, i, :w]),
                                 reads=[rs], writes=[rw2])
            for tt in range(ntiles):
                fb, rfb = next_fb(0, 4)

                def f(e, fb=fb, wt=wt, tt=tt, w=w):
                    ins = None
                    for kc in range(nkc):
                        ins = e.matmul(fb[:, :w], lhsT=hT[:, kc, tt * 128:(tt + 1) * 128], rhs=wt[:, kc, :w],
                                       start=(kc == 0), stop=(kc == nkc - 1))
                    return ins

                P.op("pe", f, reads=[R_hT, rw, rw2], writes=[rfb])
                epilogue(bi, c0, w, tt, fb, rfb)

    def cache_convert(cache, dst, R_dst, ncols, nrows=2048, padrows=(2112, 2176), extra=None):
        cw = min(ncols, 2048)
        st32 = Rot(P, "cst32", 2, [128, cw], F32)
        stb = Rot(P, "cstb", 2, [128, cw], BF16)
        k = 0
        for s in range(2):
            for rb in range(nrows // 128):
                for c0 in range(0, ncols, cw):
                    a, ra = st32.next()
                    b, rb_ = stb.next()
                    P.dma("sync", lambda e, a=a, s=s, rb=rb, c0=c0: e.dma_start(
                        out=a[:], in_=cache[s, rb * 128:(rb + 1) * 128, c0:c0 + cw]), writes=[ra], slot=ra)
                    k += 1
                    if extra is not None:
                        extra(s, rb, c0, cw, a, ra)
                    if k % 2:
                        P.op("act", lambda e, a=a, b=b: e.copy(out=b[:], in_=a[:]), reads=[ra], writes=[rb_])
                    else:
                        P.op("dve", lambda e, a=a, b=b: e.tensor_copy(out=b[:], in_=a[:]), reads=[ra], writes=[rb_])
                    P.dma("pool", lambda e, b=b, s=s, rb=rb, c0=c0: e.dma_start(
                        out=dst[s, rb * 128:(rb + 1) * 128, c0:c0 + cw], in_=b[:]), reads=[rb_], writes=[R_dst], slot=rb_)
            P.dma("pool", lambda e, s=s: e.dma_start(out=dst[s, padrows[0]:padrows[1], :],
                                                     in_=zeros_b[0:padrows[1] - padrows[0], :ncols]),
                  reads=[R_c], writes=[R_dst], slot=R_dst)

    hT = P.sb("hT", [128, KC, NTOK], BF16)
    R_hT = P.res("hT")
    hT_keep = P.sb_off
    norm_transpose_phase(x_in, hT, R_hT, "rms")
    P.sb_keep = hT_keep
    P.barrier()
    if STAGE <= 0:
        P.finish()
        return nc

    segs0 = [("aq", 0, 2048), ("akv", 2048, 1024), ("aqi", 3072, 1024), ("akiw", 4096, 80), ("ag", 4176, 2048),
             ("bq", 6224, 2048), ("bkv", 8272, 4096), ("bg", 12368, 2048)]
    blocks0 = []
    binfo0 = []
    for name, s0, sz in segs0:
        for o in range(0, sz, 512):
            w = min(512, sz - o)
            blocks0.append((s0 + o, w))
            binfo0.append((name, o, w))

    e32 = Rot(P, "e32", 3, [128, 512], F32)
    ebf = Rot(P, "ebf", 3, [128, 512], BF16)

    def store_rows(src, rsrc, c_lo, c_hi, dst, dcol, tt, Rd, kvs=None, kvs_row0=2048):
        wd = c_hi - c_lo
        if os.environ.get("MK_X2") == "1" and Rd is not R_out:
            return
        if kvs is None or tt < 8:
            P.dma(STQ, lambda e: e.dma_start(out=dst[tt * 128:(tt + 1) * 128, dcol:dcol + wd], in_=src[:, c_lo:c_hi]),
                  reads=[rsrc], writes=[Rd], slot=rsrc)
        else:
            kd, Rk = kvs
            for s in range(2):
                P.dma(STQ, lambda e, s=s: e.dma_start(out=kd[s, kvs_row0:kvs_row0 + 64, dcol:dcol + wd],
                                                         in_=src[s * 64:(s + 1) * 64, c_lo:c_hi]),
                      reads=[rsrc], writes=[Rk], slot=rsrc)

    def epi0(bi, c0, w, tt, fb, rfb):
        name, o, w = binfo0[bi]
        if os.environ.get("MK_X1") == "1":
            name = "none"
        b, rb = ebf.next()
        a, ra = e32.next()
        if (bi + tt) % 2 == 0:
            P.op("act", lambda e: e.copy(out=a[:, :w], in_=fb[:, :w]), reads=[rfb], writes=[ra])
        else:
            P.op("dve", lambda e: e.tensor_copy(out=a[:, :w], in_=fb[:, :w]), reads=[rfb], writes=[ra])
        P.op("pool", lambda e: e.tensor_copy(out=b[:, :w], in_=a[:, :w]), reads=[ra], writes=[rb])
        if name == "aq":
            store_rows(b, rb, 0, w, QA, o, tt, R_scr["QA"])
        elif name == "aqi":
            store_rows(b, rb, 0, w, QI, o, tt, R_scr["QI"])
        elif name == "ag":
            store_rows(b, rb, 0, w, GA, o, tt, R_scr["GA"])
        elif name == "bq":
            store_rows(b, rb, 0, w, QB, o, tt, R_scr["QB"])
        elif name == "bg":
            store_rows(b, rb, 0, w, GB, o, tt, R_scr["GB"])
        elif name == "akv":
            store_rows(a, ra, 0, w, o_akv, o, tt, R_out)
            store_rows(b, rb, 0, w, KVA_in, o, tt, R_scr["KVA_in"], kvs=(KVS_A, R_scr["KVS_A"]))
        elif name == "bkv":
            store_rows(a, ra, 0, w, o_bkv, o, tt, R_out)
            store_rows(b, rb, 0, w, KVB_in, o, tt, R_scr["KVB_in"], kvs=(KVS_B, R_scr["KVS_B"]))
        elif name == "akiw":
            store_rows(a, ra, 0, 64, o_aki, 0, tt, R_out)
            store_rows(a, ra, 64, 80, AW, 0, tt, R_scr["AW"])
            store_rows(b, rb, 0, 64, KI_in, 0, tt, R_scr["KI_in"], kvs=(KIS, R_scr["KIS"]))

    gemm(hT, R_hT, w_in_e, KC, blocks0, 0, epi0, wname="w_in_even")
    P.sb_keep = const_keep
    P.barrier()

    if STAGE <= 1:
        P.finish()
        return nc

    def prep_kv(group_fn, nblk, kcol, dk, vcol, dv, KT, R_KT, V, R_V):
        if "ks" not in P.cache:
            P.cache["ks"] = Rot(P, "ks", 2, [128, 8, 128], BF16)
        ks = P.cache["ks"]
        for g8 in range((nblk + 7) // 8):
            nb = min(8, nblk - g8 * 8)
            src = group_fn(g8, nb)
            t, r = ks.next()
            P.dma("sync", lambda e, t=t, src=src, nb=nb: e.dma_start(
                out=t[:, :nb, :dk], in_=src[:, :, kcol:kcol + dk].rearrange("b p d -> p b d")),
                reads=[], writes=[r], slot=r)
            if V is not None:
                P.dma("sync", lambda e, src=src, nb=nb, g8=g8: e.dma_start(
                    out=V[:, g8 * 8:g8 * 8 + nb, :], in_=src[:, :, vcol:vcol + dv].rearrange("b p d -> p b d")),
                    reads=[], writes=[R_V], slot=R_V)
            bb, rbb = next_bb()

            def f(e, t=t, bb=bb, nb=nb):
                ins = None
                for i in range(nb):
                    ins = e.transpose(out=bb[:dk, i * 128:(i + 1) * 128], in_=t[:, i, :dk], identity=ident_b[:])
                return ins

            P.op("pe", f, reads=[r, R_c], writes=[rbb])
            dst = KT[:dk, g8 * 1024:g8 * 1024 + nb * 128]
            if g8 % 2 == 0:
                P.op("act", lambda e, dst=dst, bb=bb, nb=nb: e.copy(out=dst, in_=bb[:dk, :nb * 128]), reads=[rbb], writes=[R_KT])
            else:
                P.op("dve", lambda e, dst=dst, bb=bb, nb=nb: e.tensor_copy(out=dst, in_=bb[:dk, :nb * 128]), reads=[rbb], writes=[R_KT])

    def prompt_groups(KV_all):
        v3 = KV_all.rearrange("(c t) d -> c t d", c=8)
        return lambda g8, nb: v3[:, g8 * 128:(g8 + 1) * 128, :]

    def sample_groups(KVS, s_, nrows=2176):
        v3 = KVS[s_].rearrange("(b p) d -> b p d", p=128)
        return lambda g8, nb: v3[g8 * 8:g8 * 8 + nb, :, :]

    def prep_q(Q, col, dk, QT, R_QT):
        if "qs" not in P.cache:
            P.cache["qs"] = Rot(P, "qs", 2, [128, NT, 128], BF16)
        qs, rq = P.cache["qs"].next()
        P.dma("sync", lambda e: e.dma_start(out=qs[:, :, :dk], in_=Q[:, col:col + dk].rearrange("(t p) d -> p t d", p=128)),
              writes=[rq], slot=rq)
        for (t0, nt) in ((0, 8), (8, 1)):
            bb, rbb = next_bb()

            def f(e, bb=bb, t0=t0, nt=nt):
                ins = None
                for i in range(nt):
                    ins = e.transpose(out=bb[:dk, i * 128:(i + 1) * 128], in_=qs[:, t0 + i, :dk], identity=ident_b[:])
                return ins

            P.op("pe", f, reads=[rq, R_c], writes=[rbb])
            P.op("dve", lambda e, bb=bb, t0=t0, nt=nt: e.tensor_copy(out=QT[:dk, t0 * 128:(t0 + nt) * 128], in_=bb[:dk, :nt * 128]),
                 reads=[rbb], writes=[R_QT])

    pt_rot = {}

    def pv_accumulate(nq, L, Pb, R_P, V, R_V, dv, ofb, rofb, PT):
        nblk = L // 128
        for g8 in range((nblk + 7) // 8):
            nb = min(8, nblk - g8 * 8)
            bb, rbb = next_bb()

            def f(e, bb=bb, g8=g8, nb=nb):
                ins = None
                for i in range(nb):
                    blk = g8 * 8 + i
                    ins = e.transpose(out=bb[:, i * 128:i * 128 + nq], in_=Pb[:nq, blk * 128:(blk + 1) * 128],
                                      identity=ident_b[:nq, :nq])
                return ins

            P.op("pe", f, reads=[R_P, R_c], writes=[rbb])
            pt, rpt = PT.next()
            if g8 % 2 == 0:
                P.op("act", lambda e, pt=pt, bb=bb, nb=nb: e.copy(out=pt[:, :nb * 128], in_=bb[:, :nb * 128]), reads=[rbb], writes=[rpt])
            else:
                P.op("dve", lambda e, pt=pt, bb=bb, nb=nb: e.tensor_copy(out=pt[:, :nb * 128], in_=bb[:, :nb * 128]), reads=[rbb], writes=[rpt])

            def g(e, pt=pt, g8=g8, nb=nb):
                ins = None
                for i in range(nb):
                    blk = g8 * 8 + i
                    ins = e.matmul(ofb[:nq, :dv], lhsT=pt[:, i * 128:i * 128 + nq], rhs=V[:, blk, :dv],
                                   start=(blk == 0), stop=(blk == nblk - 1))
                return ins

            P.op("pe", g, reads=[rpt, R_V], writes=[rofb])

    def softmax_pass(nq, L, LG, R_LG, Pb, R_P, st, R_st):
        P.op("dve", lambda e: e.reduce_max(out=st[:nq, 0:1], in_=LG[:nq, :L], axis=AX.X), reads=[R_LG], writes=[R_st])
        P.op("dve", lambda e: e.tensor_scalar(out=st[:nq, 1:2], in0=st[:nq, 0:1], scalar1=-1.0, scalar2=None, op0=ALU.mult),
             reads=[R_st], writes=[R_st])
        P.op("act", lambda e: e.activation(out=Pb[:nq, :L], in_=LG[:nq, :L], func=AF.Exp, bias=st[:nq, 1:2], scale=1.0,
                                           accum_out=st[:nq, 2:3]), reads=[R_LG, R_st], writes=[R_P, R_st])
        P.op("dve", lambda e: e.reciprocal(out=st[:nq, 3:4], in_=st[:nq, 2:3]), reads=[R_st], writes=[R_st])

    def logits_pass(nq, L, QT_ap, R_QT, KT, R_KT, evac):
        for kt in range((L + 511) // 512):
            w = min(512, L - kt * 512)
            fb, rfb = next_fb(0, 4)
            P.op("pe", lambda e, fb=fb, kt=kt, w=w: e.matmul(fb[:nq, :w], lhsT=QT_ap, rhs=KT[:, kt * 512:kt * 512 + w],
                                                             start=True, stop=True), reads=[R_QT, R_KT], writes=[rfb])
            evac(kt, w, fb, rfb)

    tbA_in = din("tbA", [16, 128, 1152]); tbB_in = din("tbB", [8, 128, 1152]); maskP_in = din("maskP", [128, 1152])
    tbAs_in = din("tbAs", [16, 64, 256]); tbBs_in = din("tbBs", [8, 64, 256]); maskS_in = din("maskS", [64, 256])
    constAB_in = din("constAB", [128, 24])
    lamv_in = din("lamv", [128, 4, 128]); subln_in = din("subln", [128, 256])
    SELB = dscr("SELB", [8, 128, 8192]); SELBS = dscr("SELBS", [2, 64, 2176])
    MIX = dscr("MIX", [NTOK, D])
    R_scr["SELBS"] = P.res("SELBS", persist=True)

    cache_convert(ca_kv, KVS_A, R_scr["KVS_A"], 1024)
    cache_convert(ca_ki, KIS, R_scr["KIS"], 64)
    cache_convert(cb_kv, KVS_B, R_scr["KVS_B"], 4096)
    for nm_in, nm_all, tin, tall in (("KVA_in", "KVA_all", KVA_in, KVA_all), ("KI_in", "KI_all", KI_in, KI_all),
                                     ("KVB_in", "KVB_all", KVB_in, KVB_all)):
        P.coll(lambda e, tin=tin, tall=tall: e.collective_compute("AllGather", ALU.bypass, replica_groups=[list(range(NCORES))],
                                                                  ins=[tin], outs=[tall]),
               reads=[R_scr[nm_in]], writes=[R_scr[nm_all]], slot=R_scr[nm_all])
    P.barrier()

    maskP = P.sb("maskP", [128, 1152], F32); maskS = P.sb("maskS", [64, 256], F32)
    constAB = P.sb("constAB", [128, 24], F32)
    P.dma("sync", lambda e: e.dma_start(out=maskP[:], in_=maskP_in), writes=[R_c], slot=R_c)
    P.dma("sync", lambda e: e.dma_start(out=maskS[:], in_=maskS_in), writes=[R_c], slot=R_c)
    P.dma("sync", lambda e: e.dma_start(out=constAB[:], in_=constAB_in), writes=[R_c], slot=R_c)
    att_keep = P.sb_off
    P.sb_keep = att_keep

    kiT = P.sb("kiT", [64, 8192], BF16); R_kiT = P.res("kiT")
    qiT = P.sb("qiT", [64, 16, NTOK], BF16); R_qiT = P.res("qiT")
    awt = P.sb("awt", [128, NT, 16], F32); R_awt = P.res("awt")
    aws = P.sb("aws", [64, 2, 16], F32)
    SC = P.sb("SC", [128, 8192], F32); R_SC = P.res("SC")
    SB_ = P.sb("SB", [128, 8192], BF16); R_SB = P.res("SB")
    bs = P.sb("bs", [128, 8], F32); R_bs = P.res("bs")
    relu_r = Rot(P, "relu", 3, [128, 512], F32)
    for h in range(16):
        prep_q(QI, h * 64, 64, qiT[:, h, :], R_qiT)
        P.sb_off -= 0
    P.dma("sync", lambda e: e.dma_start(out=awt[:], in_=AW.rearrange("(t p) h -> p t h", p=128)), reads=[R_scr["AW"]],
          writes=[R_awt], slot=R_awt)
    P.dma("sync", lambda e: e.dma_start(out=aws[:], in_=AW[1024:1152, :].rearrange("(s p) h -> p s h", p=64)),
          reads=[R_scr["AW"]], writes=[R_awt], slot=R_awt)

    def indexer_tile(nq, L, qcol0, kiT_ap, R_k, aw_ap, mask_ap, mlo, dst_ap, R_dst):
        for kt in range((L + 511) // 512):
            w = min(512, L - kt * 512)
            for h in range(16):
                fb, rfb = next_fb(0, 4)
                P.op("pe", lambda e, fb=fb, kt=kt, w=w, h=h: e.matmul(
                    fb[:nq, :w], lhsT=qiT[:, h, qcol0:qcol0 + nq], rhs=kiT_ap[:, kt * 512:kt * 512 + w], start=True, stop=True),
                    reads=[R_qiT, R_k], writes=[rfb])
                rl, rrl = relu_r.next()
                P.op("act", lambda e, rl=rl, fb=fb, w=w: e.activation(out=rl[:nq, :w], in_=fb[:nq, :w], func=AF.Relu),
                     reads=[rfb], writes=[rrl])
                scs = SC[:nq, kt * 512:kt * 512 + w]
                if h == 0:
                    P.op("dve", lambda e, rl=rl, w=w, scs=scs: e.tensor_scalar(out=scs, in0=rl[:nq, :w], scalar1=aw_ap[:, 0:1],
                                                                               scalar2=None, op0=ALU.mult),
                         reads=[rrl, R_awt], writes=[R_SC])
                else:
                    P.op("dve", lambda e, rl=rl, w=w, scs=scs, h=h: e.scalar_tensor_tensor(
                        out=scs, in0=rl[:nq, :w], scalar=aw_ap[:, h:h + 1], in1=scs, op0=ALU.mult, op1=ALU.add),
                        reads=[rrl, R_awt, R_SC], writes=[R_SC])
        b = bs
        P.op("dve", lambda e: e.reduce_max(out=b[:nq, 1:2], in_=SC[:nq, :L], axis=AX.X), reads=[R_SC], writes=[R_bs])
        P.op("dve", lambda e: e.tensor_reduce(out=b[:nq, 0:1], in_=SC[:nq, :L], axis=AX.X, op=ALU.min), reads=[R_SC], writes=[R_bs])
        P.op("dve", lambda e: e.tensor_scalar(out=b[:nq, 1:2], in0=b[:nq, 1:2], scalar1=1.0, scalar2=None, op0=ALU.add),
             reads=[R_bs], writes=[R_bs])
        P.op("dve", lambda e: e.tensor_scalar(out=b[:nq, 0:1], in0=b[:nq, 0:1], scalar1=-1.0, scalar2=None, op0=ALU.add),
             reads=[R_bs], writes=[R_bs])
        mw = mask_ap.shape[-1]
        P.op("dve", lambda e: e.tensor_tensor(out=SC[:nq, L - mw:L], in0=SC[:nq, L - mw:L], in1=mask_ap, op=ALU.add),
             reads=[R_SC, R_c], writes=[R_SC])
        P.op("dve", lambda e: e.memset(b[:nq, 7:8], 0.5), writes=[R_bs])
        for it in range(30):
            P.op("dve", lambda e: e.scalar_tensor_tensor(out=b[:nq, 2:3], in0=b[:nq, 0:1], scalar=b[:nq, 1:2], in1=b[:nq, 7:8],
                                                          op0=ALU.add, op1=ALU.mult), reads=[R_bs], writes=[R_bs])
            P.op("dve", lambda e: e.tensor_scalar(out=SB_[:nq, :L], in0=SC[:nq, :L], scalar1=b[:nq, 2:3], scalar2=None,
                                                  op0=ALU.is_ge, op1=ALU.add, accum_out=b[:nq, 3:4]),
                 reads=[R_SC, R_bs], writes=[R_SB, R_bs])
            P.op("dve", lambda e: e.tensor_scalar(out=b[:nq, 4:5], in0=b[:nq, 3:4], scalar1=255.5, scalar2=None, op0=ALU.is_ge),
                 reads=[R_bs], writes=[R_bs])
            P.op("dve", lambda e: e.tensor_tensor(out=b[:nq, 5:6], in0=b[:nq, 2:3], in1=b[:nq, 0:1], op=ALU.subtract),
                 reads=[R_bs], writes=[R_bs])
            P.op("dve", lambda e: e.tensor_tensor(out=b[:nq, 6:7], in0=b[:nq, 1:2], in1=b[:nq, 2:3], op=ALU.subtract),
                 reads=[R_bs], writes=[R_bs])
            P.op("dve", lambda e: e.scalar_tensor_tensor(out=b[:nq, 0:1], in0=b[:nq, 5:6], scalar=b[:nq, 4:5], in1=b[:nq, 0:1],
                                                          op0=ALU.mult, op1=ALU.add), reads=[R_bs], writes=[R_bs])
            P.op("dve", lambda e: e.scalar_tensor_tensor(out=b[:nq, 1:2], in0=b[:nq, 6:7], scalar=b[:nq, 4:5], in1=b[:nq, 2:3],
                                                          op0=ALU.mult, op1=ALU.add), reads=[R_bs], writes=[R_bs])
        P.op("dve", lambda e: e.tensor_scalar(out=SB_[:nq, :L], in0=SC[:nq, :L], scalar1=b[:nq, 0:1], scalar2=NEG,
                                              op0=ALU.is_lt, op1=ALU.mult), reads=[R_SC, R_bs], writes=[R_SB])
        P.dma("sync", lambda e: e.dma_start(out=dst_ap, in_=SB_[:nq, :L]), reads=[R_SB], writes=[R_dst], slot=R_SB)

    prep_kv(prompt_groups(KI_all), 64, 0, 64, 0, 0, kiT, R_kiT, None, None)
    for j in range(8):
        L = 1024 * (j + 1)
        indexer_tile(128, L, j * 128, kiT, R_kiT, awt[:, j, :], maskP[:, 128:1152], 0, SELB[j, :, :L], R_scr["SELB"])
    for s_ in range(2):
        prep_kv(sample_groups(KIS, s_), 17, 0, 64, 0, 0, kiT, R_kiT, None, None)
        indexer_tile(64, 2176, 1024 + 64 * s_, kiT, R_kiT, aws[:, s_, :], maskS[:, :], 0,
                     SELBS[s_, :, :], R_scr["SELBS"])
    P.barrier()
    if STAGE <= 2:
        P.finish()
        return nc

    def phase_A2():
        KT = P.sb("KT", [128, 8192], BF16); R_KT = P.res("KT")
        Vt = P.sb("Vt", [128, 64, 128], BF16); R_V = P.res("V")
        LG = P.sb("LG", [128, 8192], F32); R_LG = P.res("LG")
        Pb = P.sb("Pb", [128, 8192], BF16); R_P = P.res("Pb")
        selb = P.sb("selb", [128, 8192], BF16); R_selb = P.res("selb")
        TB = P.sb("TB", [128, 4, 1152], F32); R_TB = P.res("TB")
        TBs = P.sb("TBs", [64, 4, 256], F32); R_TBs = P.res("TBs")
        QT4 = P.sb("QT4", [128, 4, NTOK], BF16); R_QT4 = P.res("QT4")
        st_rot = Rot(P, "st", 4, [128, 8], F32)
        PT = Rot(P, "PT", 3, [128, 1024], BF16)
        gt = Rot(P, "gt", 2, [128, 512], BF16)
        sg = Rot(P, "sg", 2, [128, 512], F32)
        ob = Rot(P, "ob", 2, [128, 512], BF16)
        att_keep2 = P.sb_off

        def prep_tail(tb_in, h0, nh, mask_t, TBt, R_T, nq, cidx0):
            P.dma("sync", lambda e: e.dma_start(out=TBt[:nq, :nh, :], in_=tb_in[h0:h0 + nh].rearrange("h q k -> q h k")),
                  writes=[R_T], slot=R_T)
            for g in range(nh):
                P.op("dve", lambda e, g=g: e.scalar_tensor_tensor(out=TBt[:nq, g, :], in0=TBt[:nq, g, :],
                                                                 scalar=constAB[:nq, cidx0 + g:cidx0 + g + 1], in1=mask_t,
                                                                 op0=ALU.subtract, op1=ALU.add), reads=[R_T, R_c], writes=[R_T])

        def a_slot(nq, L, qcol0, g, tail_ap, tail_lo, selb_ap, rows_ap_fn, n, gtile, obt, r_ob, st, rst, ofb, rofb):
            def evac(kt, w, fb, rfb):
                P.op("dve", lambda e: e.scalar_tensor_tensor(out=LG[:nq, kt * 512:kt * 512 + w], in0=fb[:nq, :w], scalar=SCALE,
                                                              in1=selb_ap[:, kt * 512:kt * 512 + w], op0=ALU.mult, op1=ALU.add),
                     reads=[rfb, R_selb], writes=[R_LG])
            logits_pass(nq, L, QT4[:, g, qcol0:qcol0 + nq], R_QT4, KT, R_KT, evac)
            tw = tail_ap.shape[-1]
            P.op("pool", lambda e: e.tensor_tensor(out=LG[:nq, tail_lo:tail_lo + tw], in0=LG[:nq, tail_lo:tail_lo + tw], in1=tail_ap,
                                                   op=ALU.add), reads=[R_LG, R_TB, R_TBs], writes=[R_LG])
            softmax_pass(nq, L, LG, R_LG, Pb, R_P, st, rst)
            pv_accumulate(nq, L, Pb, R_P, Vt, R_V, 128, ofb, rofb, PT)
            P.op("dve", lambda e: e.scalar_tensor_tensor(out=obt[:nq, g * 128:(g + 1) * 128], in0=ofb[:nq, :128], scalar=st[:nq, 3:4],
                                                          in1=gtile[:nq, g * 128:(g + 1) * 128], op0=ALU.mult, op1=ALU.mult),
                 reads=[rofb, rst], writes=[r_ob])

        def load_gates_UNUSED():
            pass

        def load_gates(G, row0, nq, col0, width):
            g_, rg = gt.next()
            s_g, rsg = sg.next()
            P.dma("sync", lambda e: e.dma_start(out=g_[:nq, :width], in_=G[row0:row0 + nq, col0:col0 + width]), writes=[rg], slot=rg)
            P.op("act", lambda e: e.activation(out=s_g[:nq, :width], in_=g_[:nq, :width], func=AF.Silu), reads=[rg], writes=[rsg])
            return s_g, rsg

        ofb_i = [0]

        def next_ofb():
            k = 4 + ofb_i[0] % 2
            ofb_i[0] += 1
            return FB[k], R_fb[k]

        for n in range(4):
            prep_tail(tbA_in, 4 * n, 4, maskP[:, :], TB, R_TB, 128, 4 * n)
            prep_tail(tbAs_in, 4 * n, 4, maskS[:, :], TBs, R_TBs, 64, 4 * n)
            for g in range(4):
                prep_q(QA, (4 * n + g) * 128, 128, QT4[:, g, :], R_QT4)
            prep_kv(prompt_groups(KVA_all), 64, n * 128, 128, 512 + n * 128, 128, KT, R_KT, Vt, R_V)
            for j in range(8):
                L = 1024 * (j + 1)
                P.dma("sync", lambda e, j=j, L=L: e.dma_start(out=selb[:, :L], in_=SELB[j, :, :L]), reads=[R_scr["SELB"]],
                      writes=[R_selb], slot=R_selb)
                s_g, rsg = load_gates(GA, j * 128, 128, n * 512, 512)
                obt, r_ob = ob.next()
                for g in range(4):
                    st, rst = st_rot.next()
                    ofb, rofb = next_ofb()
                    if j == 0:
                        tail_ap, tail_lo = TB[:, g, 128:1152], 0
                    else:
                        tail_ap, tail_lo = TB[:, g, :], L - 1152
                    a_slot(128, L, j * 128, g, tail_ap, tail_lo, selb[:, :], None, n, s_g, obt, r_ob, st, rst, ofb, rofb)
                P.dma("sync", lambda e, j=j, obt=obt, n=n: e.dma_start(out=MIX[j * 128:(j + 1) * 128, n * 512:(n + 1) * 512], in_=obt[:, :]),
                      reads=[r_ob, rsg], writes=[R_scr["MIX"]], slot=r_ob)
            for s_ in range(2):
                prep_kv(sample_groups(KVS_A, s_), 17, n * 128, 128, 512 + n * 128, 128, KT, R_KT, Vt, R_V)
                P.dma("sync", lambda e, s_=s_: e.dma_start(out=selb[:64, :2176], in_=SELBS[s_, :, :]), reads=[R_scr["SELBS"]],
                      writes=[R_selb], slot=R_selb)
                r0 = 1024 + 64 * s_
                s_g, rsg = load_gates(GA, r0, 64, n * 512, 512)
                obt, r_ob = ob.next()
                for g in range(4):
                    st, rst = st_rot.next()
                    ofb, rofb = next_ofb()
                    a_slot(64, 2176, r0, g, TBs[:, g, :], 1920, selb[:64, :], None, n, s_g, obt, r_ob, st, rst, ofb, rofb)
                P.dma("sync", lambda e, r0=r0, obt=obt, n=n: e.dma_start(out=MIX[r0:r0 + 64, n * 512:(n + 1) * 512], in_=obt[:64, :]),
                      reads=[r_ob, rsg], writes=[R_scr["MIX"]], slot=r_ob)
        P.sb_keep = att_keep
        P.barrier()

    phase_A2()
    if STAGE <= 3:
        P.finish()
        return nc

    def phase_B():
        LAM_INIT0 = 0.8 - 0.6 * math.exp(-0.3 * 0)
        KT = P.sb("KT", [128, 8192], BF16); R_KT = P.res("KT")
        KT1 = P.sb("KT1", [128, 8192], BF16); R_KT1 = P.res("KT1")
        Vt = P.sb("Vt", [128, 64, 256], BF16); R_V = P.res("V")
        LG = P.sb("LG", [128, 8192], F32); R_LG = P.res("LG")
        Pb = P.sb("Pb", [128, 8192], BF16); R_P = P.res("Pb")
        TBb = P.sb("TBb", [128, 1, 1152], F32); R_TB = P.res("TBb")
        TBbs = P.sb("TBbs", [64, 1, 256], F32); R_TBs = P.res("TBbs")
        QT2 = P.sb("QT2", [128, 2, NTOK], BF16); R_QT2 = P.res("QT2")
        lamv = P.sb("lamv", [128, 4, 128], F32); subln = P.sb("subln", [128, 256], F32)
        lamt = P.sb("lamt", [128, 8], F32); R_lam = P.res("lam")
        lj = P.sb("lj", [128, 128], F32)
        st_rot = Rot(P, "st", 4, [128, 16], F32)
        PT = Rot(P, "PT", 3, [128, 1024], BF16)
        gt = Rot(P, "gt", 2, [128, 512], BF16)
        sg = Rot(P, "sg", 2, [128, 512], F32)
        ob = Rot(P, "ob", 2, [128, 512], BF16)
        o0r = Rot(P, "o0", 2, [128, 256], F32)
        o1r = Rot(P, "o1", 2, [128, 256], F32)
        sqj = P.sb("sqj", [128, 256], BF16); R_sqj = P.res("sqj")

        def prep_tail(tb_in, h0, nh, mask_t, TBt, R_T, nq, cidx0):
            P.dma("sync", lambda e: e.dma_start(out=TBt[:nq, :nh, :], in_=tb_in[h0:h0 + nh].rearrange("h q k -> q h k")),
                  writes=[R_T], slot=R_T)
            for g in range(nh):
                P.op("dve", lambda e, g=g: e.scalar_tensor_tensor(out=TBt[:nq, g, :], in0=TBt[:nq, g, :],
                                                                 scalar=constAB[:nq, cidx0 + g:cidx0 + g + 1], in1=mask_t,
                                                                 op0=ALU.subtract, op1=ALU.add), reads=[R_T, R_c], writes=[R_T])

        def load_gates(G, row0, nq, col0, width):
            g_, rg = gt.next()
            s_g, rsg = sg.next()
            P.dma("sync", lambda e: e.dma_start(out=g_[:nq, :width], in_=G[row0:row0 + nq, col0:col0 + width]), writes=[rg], slot=rg)
            P.op("act", lambda e: e.activation(out=s_g[:nq, :width], in_=g_[:nq, :width], func=AF.Silu), reads=[rg], writes=[rsg])
            return s_g, rsg
        P.dma("sync", lambda e: e.dma_start(out=lamv[:], in_=lamv_in), writes=[R_lam], slot=R_lam)
        P.dma("sync", lambda e: e.dma_start(out=subln[:], in_=subln_in), writes=[R_lam], slot=R_lam)
        for k_ in range(2):
            P.op("dve", lambda e, k_=k_: e.tensor_tensor(out=lj[:], in0=lamv[:, 2 * k_, :], in1=lamv[:, 2 * k_ + 1, :], op=ALU.mult),
                 reads=[R_lam], writes=[R_lam])
            P.op("dve", lambda e, k_=k_: e.reduce_sum(out=lamt[:, k_:k_ + 1], in_=lj[:], axis=AX.X), reads=[R_lam], writes=[R_lam])
            P.op("act", lambda e, k_=k_: e.activation(out=lamt[:, 2 + k_:3 + k_], in_=lamt[:, k_:k_ + 1], func=AF.Exp),
                 reads=[R_lam], writes=[R_lam])
        P.op("dve", lambda e: e.tensor_tensor(out=lamt[:, 4:5], in0=lamt[:, 3:4], in1=lamt[:, 2:3], op=ALU.subtract),
             reads=[R_lam], writes=[R_lam])
        P.op("dve", lambda e: e.tensor_scalar(out=lamt[:, 5:6], in0=lamt[:, 4:5], scalar1=-LAM_INIT0, scalar2=None, op0=ALU.add),
             reads=[R_lam], writes=[R_lam])

        def b_map(nq, L, qcol0, c, KTc, R_KTc, tail_ap, tail_lo, st, rst, ofb, rofb):
            def evac(kt, w, fb, rfb):
                P.op("act", lambda e: e.activation(out=LG[:nq, kt * 512:kt * 512 + w], in_=fb[:nq, :w], func=AF.Copy, scale=SCALE),
                     reads=[rfb], writes=[R_LG])
            logits_pass(nq, L, QT2[:, c, qcol0:qcol0 + nq], R_QT2, KTc, R_KTc, evac)
            tw = tail_ap.shape[-1]
            P.op("pool", lambda e: e.tensor_tensor(out=LG[:nq, tail_lo:tail_lo + tw], in0=LG[:nq, tail_lo:tail_lo + tw], in1=tail_ap,
                                                   op=ALU.add), reads=[R_LG, R_TB, R_TBs], writes=[R_LG])
            softmax_pass(nq, L, LG, R_LG, Pb, R_P, st, rst)
            pv_accumulate(nq, L, Pb, R_P, Vt, R_V, 256, ofb, rofb, PT)

        def b_tile(nq, L, qcol0, row0, hb, tail_ap, tail_lo):
            st0, rst0 = st_rot.next(); st1, rst1 = st_rot.next()
            of0, rof0 = FB[4], R_fb[4]
            of1, rof1 = FB[5], R_fb[5]
            b_map(nq, L, qcol0, 0, KT, R_KT, tail_ap, tail_lo, st0, rst0, of0, rof0)
            b_map(nq, L, qcol0, 1, KT1, R_KT1, tail_ap, tail_lo, st1, rst1, of1, rof1)
            o0, ro0 = o0r.next(); o1, ro1 = o1r.next()
            P.op("dve", lambda e: e.tensor_scalar(out=o0[:nq, :], in0=of0[:nq, :256], scalar1=st0[:nq, 3:4], scalar2=None, op0=ALU.mult),
                 reads=[rof0, rst0], writes=[ro0])
            P.op("dve", lambda e: e.tensor_tensor(out=st1[:nq, 4:5], in0=st1[:nq, 3:4], in1=lamt[:nq, 5:6], op=ALU.mult),
                 reads=[rst1, R_lam], writes=[rst1])
            P.op("dve", lambda e: e.scalar_tensor_tensor(out=o1[:nq, :], in0=of1[:nq, :256], scalar=st1[:nq, 4:5], in1=o0[:nq, :],
                                                          op0=ALU.mult, op1=ALU.add), reads=[rof1, rst1, ro0], writes=[ro1])
            P.op("act", lambda e: e.activation(out=sqj[:nq, :], in_=o1[:nq, :], func=AF.Square, accum_out=st1[:nq, 5:6]),
                 reads=[ro1], writes=[R_sqj, rst1])
            P.op("dve", lambda e: e.tensor_scalar(out=st1[:nq, 6:7], in0=st1[:nq, 5:6], scalar1=1.0 / 256, scalar2=EPS, op0=ALU.mult,
                                                  op1=ALU.add), reads=[rst1], writes=[rst1])
            P.op("act", lambda e: e.activation(out=st1[:nq, 7:8], in_=st1[:nq, 6:7], func=AF.Sqrt), reads=[rst1], writes=[rst1])
            P.op("dve", lambda e: e.reciprocal(out=st1[:nq, 8:9], in_=st1[:nq, 7:8]), reads=[rst1], writes=[rst1])
            P.op("dve", lambda e: e.tensor_scalar(out=st1[:nq, 9:10], in0=st1[:nq, 8:9], scalar1=1.0 - LAM_INIT0, scalar2=None,
                                                  op0=ALU.mult), reads=[rst1], writes=[rst1])
            s_g, rsg = load_gates(GB, row0, nq, hb * 256, 256)
            P.op("pool", lambda e: e.tensor_tensor(out=s_g[:nq, :256], in0=s_g[:nq, :256], in1=subln[:nq, :], op=ALU.mult),
                 reads=[rsg, R_lam], writes=[rsg])
            obt, r_ob = ob.next()
            P.op("dve", lambda e: e.scalar_tensor_tensor(out=obt[:nq, :256], in0=o1[:nq, :], scalar=st1[:nq, 9:10], in1=s_g[:nq, :256],
                                                          op0=ALU.mult, op1=ALU.mult), reads=[ro1, rst1, rsg], writes=[r_ob])
            P.dma("sync", lambda e: e.dma_start(out=MIX[row0:row0 + nq, 2048 + hb * 256:2048 + (hb + 1) * 256], in_=obt[:nq, :256]),
                  reads=[r_ob], writes=[R_scr["MIX"]], slot=r_ob)

        for hb in range(8):
            prep_tail(tbB_in, hb, 1, maskP[:, :], TBb, R_TB, 128, 16 + hb)
            prep_tail(tbBs_in, hb, 1, maskS[:, :], TBbs, R_TBs, 64, 16 + hb)
            for c in range(2):
                prep_q(QB, hb * 256 + c * 128, 128, QT2[:, c, :], R_QT2)
            prep_kv(prompt_groups(KVB_all), 64, hb * 256, 128, 2048 + hb * 256, 256, KT, R_KT, Vt, R_V)
            prep_kv(prompt_groups(KVB_all), 64, hb * 256 + 128, 128, 0, 0, KT1, R_KT1, None, None)
            for j in range(8):
                L = 1024 * (j + 1)
                if j == 0:
                    b_tile(128, L, 0, 0, hb, TBb[:, 0, 128:1152], 0)
                else:
                    b_tile(128, L, j * 128, j * 128, hb, TBb[:, 0, :], L - 1152)
            for s_ in range(2):
                prep_kv(sample_groups(KVS_B, s_), 17, hb * 256, 128, 2048 + hb * 256, 256, KT, R_KT, Vt, R_V)
                prep_kv(sample_groups(KVS_B, s_), 17, hb * 256 + 128, 128, 0, 0, KT1, R_KT1, None, None)
                b_tile(64, 2176, 1024 + 64 * s_, 1024 + 64 * s_, hb, TBbs[:, 0, :], 1920)
        P.sb_keep = const_keep
        P.barrier()

    phase_B()
    if STAGE <= 4:
        P.finish()
        return nc

    MO = dscr("MO", [NTOK, D], F32); X1 = dscr("X1", [NTOK, D], F32); SGd = dscr("SGd", [NTOK, D], F32)
    X2 = dscr("X2", [NTOK, D], F32)
    R_scr["X2"] = P.res("X2", persist=True)
    blocks8 = [(i * 512, 512) for i in range(8)]

    def finish_layer(li, w_out, x_src, R_xsrc, x_dst, R_xdst):
        hTm = P.sb("hT", [128, KC, NTOK], BF16); R_hTm = P.res("hTm")
        keep = P.sb_off
        P.sb_keep = keep
        norm_transpose_phase(MIX, hTm, R_hTm, "raw", reads=[R_scr["MIX"]])
        P.barrier()
        e32f = Rot(P, "e32f", 3, [128, 512], F32)

        def epi_mo(bi, c0, w, tt, fb, rfb):
            a, ra = e32f.next()
            if (bi + tt) % 2 == 0:
                P.op("act", lambda e: e.copy(out=a[:, :w], in_=fb[:, :w]), reads=[rfb], writes=[ra])
            else:
                P.op("dve", lambda e: e.tensor_copy(out=a[:, :w], in_=fb[:, :w]), reads=[rfb], writes=[ra])
            P.dma(STQ, lambda e: e.dma_start(out=MO[tt * 128:(tt + 1) * 128, c0:c0 + w], in_=a[:, :w]), reads=[ra],
                  writes=[R_scr["MO"]], slot=ra)

        gemm(hTm, R_hTm, w_out, KC, blocks8, None, epi_mo, wname=("w_out_even" if li == 0 else "w_out_odd"))
        P.barrier()
        gp = P.sb("gp", [128, D], F32); R_gp = P.res("gp")
        P.dma("sync", lambda e: e.dma_start(out=gp[:], in_=gpost_in[li]), writes=[R_gp], slot=R_gp)
        mo_r = Rot(P, "mo", 2, [128, D], F32)
        x_r = Rot(P, "xr", 2, [128, D], F32)
        xb_r = Rot(P, "xbr", 2, [128, D], BF16)
        st2 = Rot(P, "st2", 2, [128, 4], F32)
        junk = P.sb("junk2", [128, D], BF16); R_junk = P.res("junk2")
        for tt in range(NT):
            m, rm = mo_r.next(); xx, rx = x_r.next(); xb, rxb = xb_r.next(); s2, rs2 = st2.next()
            rows = slice(tt * 128, (tt + 1) * 128)
            P.dma("sync", lambda e, m=m, rows=rows: e.dma_start(out=m[:], in_=MO[rows, :]), reads=[R_scr["MO"]], writes=[rm], slot=rm)
            P.dma("sync", lambda e, xx=xx, rows=rows: e.dma_start(out=xx[:], in_=x_src[rows, :]), reads=[R_xsrc], writes=[rx], slot=rx)
            P.op("act", lambda e, m=m, s2=s2: e.activation(out=junk[:], in_=m[:], func=AF.Square, accum_out=s2[:, 0:1]),
                 reads=[rm], writes=[R_junk, rs2])
            P.op("dve", lambda e, s2=s2: e.tensor_scalar(out=s2[:, 1:2], in0=s2[:, 0:1], scalar1=1.0 / D, scalar2=EPS, op0=ALU.mult,
                                                         op1=ALU.add), reads=[rs2], writes=[rs2])
            P.op("act", lambda e, s2=s2: e.activation(out=s2[:, 2:3], in_=s2[:, 1:2], func=AF.Sqrt), reads=[rs2], writes=[rs2])
            P.op("dve", lambda e, s2=s2: e.reciprocal(out=s2[:, 3:4], in_=s2[:, 2:3]), reads=[rs2], writes=[rs2])
            P.op("dve", lambda e, m=m, s2=s2: e.scalar_tensor_tensor(out=m[:], in0=m[:], scalar=s2[:, 3:4], in1=gp[:], op0=ALU.mult,
                                                                     op1=ALU.mult), reads=[rm, rs2, R_gp], writes=[rm])
            P.op("pool", lambda e, m=m, xx=xx: e.tensor_tensor(out=xx[:], in0=xx[:], in1=m[:], op=ALU.add), reads=[rm, rx], writes=[rx])
            P.dma(STQ, lambda e, xx=xx, rows=rows: e.dma_start(out=X1[rows, :], in_=xx[:]), reads=[rx], writes=[R_scr["X1"]], slot=rx)
            P.op("act", lambda e, xx=xx, xb=xb: e.copy(out=xb[:], in_=xx[:]), reads=[rx], writes=[rxb])
            transpose_rows(xb, rxb, hTm, R_hTm, tt)
        P.barrier()
        e32g = Rot(P, "e32g", 3, [128, 512], F32)

        def epi_sg(bi, c0, w, tt, fb, rfb):
            a, ra = e32g.next()
            P.op("act", lambda e: e.activation(out=a[:, :w], in_=fb[:, :w], func=AF.Sigmoid), reads=[rfb], writes=[ra])
            P.dma(STQ, lambda e: e.dma_start(out=SGd[tt * 128:(tt + 1) * 128, c0:c0 + w], in_=a[:, :w]), reads=[ra],
                  writes=[R_scr["SG"]], slot=ra)

        gemm(hTm, R_hTm, w_plg[li], KC, blocks8, None, epi_sg, wname="w_pl_gate%d" % li)
        P.sb_keep = const_keep
        P.barrier()
        gpl = P.sb("gpl", [128, D], F32); R_gpl = P.res("gpl")
        P.dma("sync", lambda e: e.dma_start(out=gpl[:], in_=gpl_in[li]), writes=[R_gpl], slot=R_gpl)
        wpb = P.sb("wpb", [128, 2, D], BF16); R_wpb = P.res("wpb")
        wst = Rot(P, "wst", 2, [128, D], F32)
        for kc in range(2):
            t, r = wst.next()
            P.dma("sync", lambda e, t=t, kc=kc: e.dma_start(out=t[:], in_=w_plp[li, kc * 128:(kc + 1) * 128, :]), writes=[r], slot=r)
            P.op("dve", lambda e, t=t, kc=kc: e.tensor_copy(out=wpb[:, kc, :], in_=t[:]), reads=[r], writes=[R_wpb])
        pt32 = Rot(P, "pt32", 2, [128, 256], F32)
        ptb = Rot(P, "ptb", 2, [128, 256], BF16)
        pTt = Rot(P, "pTt", 2, [128, 256], BF16)
        e_r = Rot(P, "er", 2, [128, D], F32)
        sg_r = Rot(P, "sgr", 2, [128, D], F32)
        x1_r = Rot(P, "x1r", 2, [128, D], F32)
        st3 = Rot(P, "st3", 2, [128, 4], F32)
        junk = P.sb("junk3", [128, D], BF16); R_junk = P.res("junk3")
        for tt in range(NT):
            rows = slice(tt * 128, (tt + 1) * 128)
            a, ra = pt32.next(); b, rb = ptb.next(); pT, rpT = pTt.next()
            P.dma("sync", lambda e, a=a, rows=rows: e.dma_start(out=a[:], in_=p_in[li, rows, :]), writes=[ra], slot=ra)
            P.op("dve", lambda e, a=a, b=b: e.tensor_copy(out=b[:], in_=a[:]), reads=[ra], writes=[rb])
            bb, rbb = next_bb()

            def f(e, bb=bb, b=b):
                ins = None
                for i in range(2):
                    ins = e.transpose(out=bb[:, i * 128:(i + 1) * 128], in_=b[:, i * 128:(i + 1) * 128], identity=ident_b[:])
                return ins

            P.op("pe", f, reads=[rb, R_c], writes=[rbb])
            P.op("dve", lambda e, pT=pT, bb=bb: e.tensor_copy(out=pT[:], in_=bb[:, :256]), reads=[rbb], writes=[rpT])
            et, ret = e_r.next(); sgt, rsgt = sg_r.next(); x1t, rx1 = x1_r.next(); s3, rs3 = st3.next()
            P.dma("sync", lambda e, sgt=sgt, rows=rows: e.dma_start(out=sgt[:], in_=SGd[rows, :]), reads=[R_scr["SG"]], writes=[rsgt], slot=rsgt)
            P.dma("sync", lambda e, x1t=x1t, rows=rows: e.dma_start(out=x1t[:], in_=X1[rows, :]), reads=[R_scr["X1"]], writes=[rx1], slot=rx1)
            for cb in range(8):
                fb, rfb = next_fb(0, 6)

                def g(e, fb=fb, pT=pT, cb=cb):
                    ins = None
                    for kc in range(2):
                        ins = e.matmul(fb[:, :512], lhsT=pT[:, kc * 128:(kc + 1) * 128], rhs=wpb[:, kc, cb * 512:(cb + 1) * 512],
                                       start=(kc == 0), stop=(kc == 1))
                    return ins

                P.op("pe", g, reads=[rpT, R_wpb], writes=[rfb])
                if cb % 2 == 0:
                    P.op("act", lambda e, et=et, fb=fb, cb=cb: e.copy(out=et[:, cb * 512:(cb + 1) * 512], in_=fb[:, :512]),
                         reads=[rfb], writes=[ret])
                else:
                    P.op("dve", lambda e, et=et, fb=fb, cb=cb: e.tensor_copy(out=et[:, cb * 512:(cb + 1) * 512], in_=fb[:, :512]),
                         reads=[rfb], writes=[ret])
            P.op("act", lambda e, et=et, s3=s3: e.activation(out=junk[:], in_=et[:], func=AF.Square, accum_out=s3[:, 0:1]),
                 reads=[ret], writes=[R_junk, rs3])
            P.op("dve", lambda e, s3=s3: e.tensor_scalar(out=s3[:, 1:2], in0=s3[:, 0:1], scalar1=1.0 / D, scalar2=EPS, op0=ALU.mult,
                                                         op1=ALU.add), reads=[rs3], writes=[rs3])
            P.op("act", lambda e, s3=s3: e.activation(out=s3[:, 2:3], in_=s3[:, 1:2], func=AF.Sqrt), reads=[rs3], writes=[rs3])
            P.op("dve", lambda e, s3=s3: e.reciprocal(out=s3[:, 3:4], in_=s3[:, 2:3]), reads=[rs3], writes=[rs3])
            P.op("dve", lambda e, et=et, s3=s3: e.scalar_tensor_tensor(out=et[:], in0=et[:], scalar=s3[:, 3:4], in1=gpl[:], op0=ALU.mult,
                                                                       op1=ALU.mult), reads=[ret, rs3, R_gpl], writes=[ret])
            P.op("pool", lambda e, et=et, sgt=sgt: e.tensor_tensor(out=et[:], in0=et[:], in1=sgt[:], op=ALU.mult),
                 reads=[ret, rsgt], writes=[ret])
            P.op("pool", lambda e, et=et, x1t=x1t: e.tensor_tensor(out=x1t[:], in0=x1t[:], in1=et[:], op=ALU.add),
                 reads=[ret, rx1], writes=[rx1])
            P.dma(STQ, lambda e, x1t=x1t, rows=rows: e.dma_start(out=x_dst[rows, :], in_=x1t[:]), reads=[rx1], writes=[R_xdst], slot=rx1)
        P.barrier()

    finish_layer(0, w_out_e, x_in, R_c, X2, R_scr["X2"])
    if DBGOUT:
        dbg_mix = dout("dbg_mix", [NTOK, D], BF16); dbg_x2 = dout("dbg_x2", [NTOK, D]); dbg_selbs = dout("dbg_selbs", [2, 64, 2176], BF16)
        dbg_mo = dout("dbg_mo", [NTOK, D])
        db16 = Rot(P, "db16", 2, [128, D], BF16)
        db32 = Rot(P, "db32", 2, [128, D], F32)
        for tt in range(NT):
            rows = slice(tt * 128, (tt + 1) * 128)
            t, r = db16.next()
            P.dma("sync", lambda e, t=t, rows=rows: e.dma_start(out=t[:], in_=MIX[rows, :]), reads=[R_scr["MIX"]], writes=[r], slot=r)
            P.dma("sync", lambda e, t=t, rows=rows: e.dma_start(out=dbg_mix[rows, :], in_=t[:]), reads=[r], writes=[R_out], slot=r)
            for src_, dst_, rk in ((X2, dbg_x2, "X2"), (MO, dbg_mo, "MO")):
                t, r = db32.next()
                P.dma("sync", lambda e, t=t, rows=rows, src_=src_: e.dma_start(out=t[:], in_=src_[rows, :]), reads=[R_scr[rk]], writes=[r], slot=r)
                P.dma("sync", lambda e, t=t, rows=rows, dst_=dst_: e.dma_start(out=dst_[rows, :], in_=t[:]), reads=[r], writes=[R_out], slot=r)
        for s_ in range(2):
            t, r = db16.next()
            P.dma("sync", lambda e, t=t, s_=s_: e.dma_start(out=t[:64, :2176], in_=SELBS[s_]), reads=[R_scr["SELBS"]], writes=[r], slot=r)
            P.dma("sync", lambda e, t=t, s_=s_: e.dma_start(out=dbg_selbs[s_], in_=t[:64, :2176]), reads=[r], writes=[R_out], slot=r)
        P.barrier()
    if STAGE <= 5:
        P.finish()
        return nc

    QC = dscr("QC", [NTOK, 2048]); GC = dscr("GC", [NTOK, 2048]); QD = dscr("QD", [NTOK, 2048]); GD = dscr("GD", [NTOK, 2048])
    KVC_in = dscr("KVC_in", [1024, 4096]); KVD_in = dscr("KVD_in", [1024, 4096])
    KVC_all = dscr("KVC_all", [8192, 4096]); KVD_all = dscr("KVD_all", [8192, 4096])
    KVS_C = dscr("KVS_C", [2, 2176, 4096]); KVS_D = dscr("KVS_D", [2, 640, 4096])
    for k_ in ("QC", "GC", "QD", "GD", "KVC_in", "KVD_in", "KVC_all", "KVD_all", "KVS_C", "KVS_D"):
        R_scr[k_] = P.res(k_, persist=True)
    maskC_in = din("maskC", [128, 1024]); maskCs_in = din("maskCs", [64, 128])
    tbD_in = din("tbD", [16, 128, 1536]); maskD_in = din("maskD", [128, 1536])
    tbDs_in = din("tbDs", [16, 64, 640]); maskDs_in = din("maskDs", [64, 640])

    def layer1_inproj():
        hT1 = P.sb("hT1", [128, KC, NTOK], BF16); R_hT1 = P.res("hT1")
        P.sb_keep = P.sb_off
        norm_transpose_phase(X2, hT1, R_hT1, "rms", reads=[R_scr["X2"]])
        P.barrier()
        segs1 = [("cq", 0, 2048), ("ckv", 2048, 4096), ("cg", 6144, 2048), ("dq", 8192, 2048), ("dkv", 10240, 4096), ("dg", 14336, 2048)]
        blocks1, binfo1 = [], []
        for name, s0, sz in segs1:
            for o in range(0, sz, 512):
                blocks1.append((s0 + o, 512)); binfo1.append((name, o, 512))
        e32_ = Rot(P, "e32", 3, [128, 512], F32)
        ebf_ = Rot(P, "ebf", 3, [128, 512], BF16)

        def epi1(bi, c0, w, tt, fb, rfb):
            name, o, w = binfo1[bi]
            b, rb = ebf_.next()
            a, ra = e32_.next()
            if (bi + tt) % 2 == 0:
                P.op("act", lambda e: e.copy(out=a[:, :w], in_=fb[:, :w]), reads=[rfb], writes=[ra])
            else:
                P.op("dve", lambda e: e.tensor_copy(out=a[:, :w], in_=fb[:, :w]), reads=[rfb], writes=[ra])
            P.op("pool", lambda e: e.tensor_copy(out=b[:, :w], in_=a[:, :w]), reads=[ra], writes=[rb])
            if name == "cq":
                store_rows(b, rb, 0, w, QC, o, tt, R_scr["QC"])
            elif name == "cg":
                store_rows(b, rb, 0, w, GC, o, tt, R_scr["GC"])
            elif name == "dq":
                store_rows(b, rb, 0, w, QD, o, tt, R_scr["QD"])
            elif name == "dg":
                store_rows(b, rb, 0, w, GD, o, tt, R_scr["GD"])
            elif name == "ckv":
                store_rows(a, ra, 0, w, o_ckv, o, tt, R_out)
                store_rows(b, rb, 0, w, KVC_in, o, tt, R_scr["KVC_in"], kvs=(KVS_C, R_scr["KVS_C"]))
            elif name == "dkv":
                store_rows(a, ra, 0, w, o_dkv, o, tt, R_out)
                store_rows(b, rb, 0, w, KVD_in, o, tt, R_scr["KVD_in"], kvs=(KVS_D, R_scr["KVS_D"]), kvs_row0=512)
                if tt == 8:
                    for s_ in range(2):
                        P.dma(STQ, lambda e, s_=s_: e.dma_start(out=o_ds[s_, 448:512, o:o + w], in_=a[s_ * 64:(s_ + 1) * 64, :w]),
                              reads=[ra], writes=[R_out], slot=ra)

        gemm(hT1, R_hT1, w_in_o, KC, blocks1, 1, epi1, wname="w_in_odd")
        P.sb_keep = const_keep
        P.barrier()

    layer1_inproj()
    R_ods = P.res("ods", persist=True)
    def ods_extra(s_, rb, c0, cw, a, ra):
        if rb == 0:
            P.dma(STQ, lambda e: e.dma_start(out=o_ds[s_, 0:64, c0:c0 + cw], in_=a[64:128, :]), reads=[ra], writes=[R_out], slot=ra)
        else:
            P.dma(STQ, lambda e: e.dma_start(out=o_ds[s_, rb * 128 - 64:rb * 128 + 64, c0:c0 + cw], in_=a[:, :]), reads=[ra],
                  writes=[R_out], slot=ra)

    cache_convert(cc_kv, KVS_C, R_scr["KVS_C"], 4096)
    cache_convert(cd_kv, KVS_D, R_scr["KVS_D"], 4096, nrows=512, padrows=(576, 640), extra=ods_extra)
    for nm_in, nm_all, tin, tall in (("KVC_in", "KVC_all", KVC_in, KVC_all), ("KVD_in", "KVD_all", KVD_in, KVD_all)):
        P.coll(lambda e, tin=tin, tall=tall: e.collective_compute("AllGather", ALU.bypass, replica_groups=[list(range(NCORES))],
                                                                  ins=[tin], outs=[tall]),
               reads=[R_scr[nm_in]], writes=[R_scr[nm_all]], slot=R_scr[nm_all])
    P.barrier()
    if STAGE <= 6:
        P.finish()
        return nc

    def phase_C():
        KT = P.sb("KT", [128, 8192], BF16); R_KT = P.res("KT")
        Vt = P.sb("Vt", [128, 64, 128], BF16); R_V = P.res("V")
        LG = P.sb("LG", [128, 8192], F32); R_LG = P.res("LG")
        SP = P.sb("SP", [128, 8192], F32); R_SP = P.res("SP")
        Pb = P.sb("Pb", [128, 8192], BF16); R_P = P.res("Pb")
        ones = P.sb("ones", [128, 8192], BF16)
        QT1 = P.sb("QT1", [128, NTOK], BF16); R_QT1 = P.res("QT1")
        mC = P.sb("mC", [128, 1024], F32); mCn = P.sb("mCn", [128, 1024], F32)
        mCs = P.sb("mCs", [64, 128], F32); mCsn = P.sb("mCsn", [64, 128], F32)
        R_m = P.res("mC")
        st_rot = Rot(P, "st", 4, [128, 8], F32)
        PT = Rot(P, "PT", 3, [128, 1024], BF16)
        gt = Rot(P, "gt", 2, [128, 128], BF16)
        sg = Rot(P, "sg", 2, [128, 128], F32)
        ob = Rot(P, "ob", 2, [128, 128], BF16)
        P.op("pool", lambda e: e.memset(ones[:], 1.0), writes=[R_m])
        P.dma("sync", lambda e: e.dma_start(out=mC[:], in_=maskC_in), writes=[R_m], slot=R_m)
        P.dma("sync", lambda e: e.dma_start(out=mCs[:], in_=maskCs_in), writes=[R_m], slot=R_m)
        P.op("dve", lambda e: e.tensor_scalar(out=mCn[:], in0=mC[:], scalar1=-1.0, scalar2=-NEG, op0=ALU.add, op1=ALU.mult),
             reads=[R_m], writes=[R_m])
        P.op("dve", lambda e: e.tensor_scalar(out=mCsn[:], in0=mCs[:], scalar1=-1.0, scalar2=-NEG, op0=ALU.add, op1=ALU.mult),
             reads=[R_m], writes=[R_m])

        def c_tile(nq, L, qcol0, row0, h, m01, mneg, tw):
            st, rst = st_rot.next()
            ofb, rofb = FB[4 + (st_rot.i % 2)], R_fb[4 + (st_rot.i % 2)]

            def evac(kt, w, fb, rfb):
                P.op("dve", lambda e: e.tensor_scalar(out=LG[:nq, kt * 512:kt * 512 + w], in0=fb[:nq, :w], scalar1=SCALE, scalar2=30.0,
                                                      op0=ALU.mult, op1=ALU.min), reads=[rfb], writes=[R_LG])
            logits_pass(nq, L, QT1[:, qcol0:qcol0 + nq], R_QT1, KT, R_KT, evac)
            P.op("act", lambda e: e.activation(out=SP[:nq, :L], in_=LG[:nq, :L], func=AF.Exp), reads=[R_LG], writes=[R_SP])
            P.op("act", lambda e: e.activation(out=SP[:nq, :L], in_=SP[:nq, :L], func=AF.Ln, bias=1.0), reads=[R_SP], writes=[R_SP])
            P.op("dve", lambda e: e.tensor_tensor(out=LG[:nq, :L], in0=LG[:nq, :L], in1=SP[:nq, :L], op=ALU.subtract),
                 reads=[R_LG, R_SP], writes=[R_LG])
            P.op("pool", lambda e: e.tensor_tensor(out=SP[:nq, L - tw:L], in0=SP[:nq, L - tw:L], in1=m01, op=ALU.mult),
                 reads=[R_SP, R_m, R_LG], writes=[R_SP])
            P.op("pool", lambda e: e.tensor_tensor(out=LG[:nq, L - tw:L], in0=LG[:nq, L - tw:L], in1=mneg, op=ALU.add),
                 reads=[R_LG, R_m], writes=[R_LG])
            P.op("dve", lambda e: e.tensor_tensor_scan(out=SP[:nq, :L], data0=ones[:nq, :L], data1=SP[:nq, :L], initial=0.0,
                                                       op0=ALU.mult, op1=ALU.add), reads=[R_SP, R_m], writes=[R_SP])
            P.op("dve", lambda e: e.tensor_scalar(out=st[:nq, 1:2], in0=SP[:nq, L - 1:L], scalar1=-1.0, scalar2=None, op0=ALU.mult),
                 reads=[R_SP], writes=[rst])
            P.op("pool", lambda e: e.tensor_tensor(out=LG[:nq, :L], in0=LG[:nq, :L], in1=SP[:nq, :L], op=ALU.add),
                 reads=[R_LG, R_SP], writes=[R_LG])
            P.op("act", lambda e: e.activation(out=Pb[:nq, :L], in_=LG[:nq, :L], func=AF.Exp, bias=st[:nq, 1:2], scale=1.0),
                 reads=[R_LG, rst], writes=[R_P])
            pv_accumulate(nq, L, Pb, R_P, Vt, R_V, 128, ofb, rofb, PT)
            g_, rg = gt.next(); s_g, rsg = sg.next(); obt, r_ob = ob.next()
            P.dma("sync", lambda e: e.dma_start(out=g_[:nq, :], in_=GC[row0:row0 + nq, h * 128:(h + 1) * 128]), reads=[R_scr["GC"]],
                  writes=[rg], slot=rg)
            P.op("act", lambda e: e.activation(out=s_g[:nq, :], in_=g_[:nq, :], func=AF.Silu), reads=[rg], writes=[rsg])
            P.op("dve", lambda e: e.tensor_tensor(out=obt[:nq, :], in0=ofb[:nq, :128], in1=s_g[:nq, :], op=ALU.mult),
                 reads=[rofb, rsg], writes=[r_ob])
            P.dma("sync", lambda e: e.dma_start(out=MIX[row0:row0 + nq, h * 128:(h + 1) * 128], in_=obt[:nq, :]), reads=[r_ob],
                  writes=[R_scr["MIX"]], slot=r_ob)

        for h in range(16):
            prep_q(QC, h * 128, 128, QT1, R_QT1)
            prep_kv(prompt_groups(KVC_all), 64, h * 128, 128, 2048 + h * 128, 128, KT, R_KT, Vt, R_V)
            for j in range(8):
                c_tile(128, 1024 * (j + 1), j * 128, j * 128, h, mC[:, :], mCn[:, :], 1024)
            for s_ in range(2):
                prep_kv(sample_groups(KVS_C, s_), 17, h * 128, 128, 2048 + h * 128, 128, KT, R_KT, Vt, R_V)
                c_tile(64, 2176, 1024 + 64 * s_, 1024 + 64 * s_, h, mCs[:, :], mCsn[:, :], 128)
        P.sb_keep = const_keep
        P.barrier()

    def phase_D():
        KT = P.sb("KT", [128, 8192], BF16); R_KT = P.res("KT")
        Vt = P.sb("Vt", [128, 64, 128], BF16); R_V = P.res("V")
        LG = P.sb("LG", [128, 1536], F32); R_LG = P.res("LG")
        Pb = P.sb("Pb", [128, 1536], BF16); R_P = P.res("Pb")
        QT1 = P.sb("QT1", [128, NTOK], BF16); R_QT1 = P.res("QT1")
        TBd = P.sb("TBd", [128, 1536], F32); TBds = P.sb("TBds", [64, 640], F32)
        mD = P.sb("mD", [128, 1536], F32); mDs = P.sb("mDs", [64, 640], F32)
        R_T = P.res("TBd"); R_m = P.res("mD")
        st_rot = Rot(P, "st", 4, [128, 8], F32)
        PT = Rot(P, "PT", 3, [128, 1024], BF16)
        gt = Rot(P, "gt", 2, [128, 128], BF16)
        sg = Rot(P, "sg", 2, [128, 128], F32)
        ob = Rot(P, "ob", 2, [128, 128], BF16)
        P.dma("sync", lambda e: e.dma_start(out=mD[:], in_=maskD_in), writes=[R_m], slot=R_m)
        P.dma("sync", lambda e: e.dma_start(out=mDs[:], in_=maskDs_in), writes=[R_m], slot=R_m)

        def d_tile(nq, Lz, k0, qcol0, row0, h, tail_ap):
            st, rst = st_rot.next()
            ofb, rofb = FB[4 + (st_rot.i % 2)], R_fb[4 + (st_rot.i % 2)]

            def evac(kt, w, fb, rfb):
                P.op("act", lambda e: e.activation(out=LG[:nq, kt * 512:kt * 512 + w], in_=fb[:nq, :w], func=AF.Copy, scale=SCALE),
                     reads=[rfb], writes=[R_LG])
            logits_pass(nq, Lz, QT1[:, qcol0:qcol0 + nq], R_QT1, KT[:, k0:k0 + Lz], R_KT, evac)
            P.op("pool", lambda e: e.tensor_tensor(out=LG[:nq, :Lz], in0=LG[:nq, :Lz], in1=tail_ap, op=ALU.add),
                 reads=[R_LG, R_T], writes=[R_LG])
            softmax_pass(nq, Lz, LG, R_LG, Pb, R_P, st, rst)
            pv_accumulate(nq, Lz, Pb, R_P, Vt[:, k0 // 128:(k0 + Lz) // 128, :], R_V, 128, ofb, rofb, PT)
            g_, rg = gt.next(); s_g, rsg = sg.next(); obt, r_ob = ob.next()
            P.dma("sync", lambda e: e.dma_start(out=g_[:nq, :], in_=GD[row0:row0 + nq, h * 128:(h + 1) * 128]), reads=[R_scr["GD"]],
                  writes=[rg], slot=rg)
            P.op("act", lambda e: e.activation(out=s_g[:nq, :], in_=g_[:nq, :], func=AF.Silu), reads=[rg], writes=[rsg])
            P.op("dve", lambda e: e.scalar_tensor_tensor(out=obt[:nq, :], in0=ofb[:nq, :128], scalar=st[:nq, 3:4], in1=s_g[:nq, :],
                                                          op0=ALU.mult, op1=ALU.mult), reads=[rofb, rst, rsg], writes=[r_ob])
            P.dma("sync", lambda e: e.dma_start(out=MIX[row0:row0 + nq, 2048 + h * 128:2048 + (h + 1) * 128], in_=obt[:nq, :]),
                  reads=[r_ob], writes=[R_scr["MIX"]], slot=r_ob)

        for h in range(16):
            P.dma("sync", lambda e, h=h: e.dma_start(out=TBd[:], in_=tbD_in[h]), writes=[R_T], slot=R_T)
            P.dma("sync", lambda e, h=h: e.dma_start(out=TBds[:], in_=tbDs_in[h]), writes=[R_T], slot=R_T)
            P.op("dve", lambda e: e.tensor_tensor(out=TBd[:], in0=TBd[:], in1=mD[:], op=ALU.add), reads=[R_T, R_m], writes=[R_T])
            P.op("dve", lambda e: e.tensor_tensor(out=TBds[:], in0=TBds[:], in1=mDs[:], op=ALU.add), reads=[R_T, R_m], writes=[R_T])
            prep_q(QD, h * 128, 128, QT1, R_QT1)
            prep_kv(prompt_groups(KVD_all), 64, h * 128, 128, 2048 + h * 128, 128, KT, R_KT, Vt, R_V)
            for j in range(8):
                if j == 0:
                    d_tile(128, 1024, 0, 0, 0, h, TBd[:, 512:1536])
                else:
                    d_tile(128, 1536, 1024 * j - 512, j * 128, j * 128, h, TBd[:, :])
            for s_ in range(2):
                prep_kv(sample_groups(KVS_D, s_), 5, h * 128, 128, 2048 + h * 128, 128, KT, R_KT, Vt, R_V)
                d_tile(64, 640, 0, 1024 + 64 * s_, 1024 + 64 * s_, h, TBds[:, :])
        P.sb_keep = const_keep
        P.barrier()

    phase_C()
    phase_D()
    if STAGE <= 7:
        P.finish()
        return nc
    finish_layer(1, w_out_o, X2, R_scr["X2"], y_out, R_out)
    P.finish()
    return nc


_CACHE = {}


def kernel(x_prompt, x_sample, p_prompt, p_sample, cache_a_kv, cache_a_kidx, cache_b_kv, cache_c_kv,
           cache_d_kv, norm_pre, norm_post, w_in_even, w_out_even, t5_bias, diff_lambda, diff_subln,
           w_in_odd, w_out_odd, d_rel_bias, w_pl_proj, pl_norm, w_pl_gate):
    f = lambda a: np.ascontiguousarray(np.asarray(a, dtype=np.float32))
    x_prompt, x_sample, p_prompt, p_sample = f(x_prompt), f(x_sample), f(p_prompt), f(p_sample)
    if "nc" not in _CACHE:
        _CACHE["nc"] = build_program()
    nc = _CACHE["nc"]

    xp = x_prompt[0].reshape(8, 8, 128, D)
    pp = p_prompt[:, 0].reshape(2, 8, 8, 128, 256)
    ident = np.eye(128, dtype=np.float32)
    gpre = f(norm_pre).reshape(2, KC, 128).transpose(0, 2, 1).copy()
    wfull = {"w_in_even": f(w_in_even)[0], "w_out_even": f(w_out_even)[0], "w_in_odd": f(w_in_odd)[0],
             "w_out_odd": f(w_out_odd)[0], "w_pl_gate0": f(w_pl_gate)[0], "w_pl_gate1": f(w_pl_gate)[1]}
    shared = {"w_pl_proj": f(w_pl_proj), "gpre": gpre, "ident": ident,
              "gpost": np.ascontiguousarray(np.broadcast_to(f(norm_post)[:, None, :], (2, 128, D))),
              "gpl": np.ascontiguousarray(np.broadcast_to(f(pl_norm)[:, None, :], (2, 128, D)))}
    t5 = f(t5_bias)
    lamv = np.ascontiguousarray(np.broadcast_to(f(diff_lambda)[0][None], (128, 4, 128)))
    subln = np.ascontiguousarray(np.broadcast_to(f(diff_subln)[0][None], (128, 256)))
    constAB = np.ascontiguousarray(np.broadcast_to(t5[15][None], (128, 24)))
    shared.update({"lamv": lamv, "subln": subln, "constAB": constAB})
    qs_ = 2048 + np.arange(64)
    ks_ = np.arange(1920, 2176)
    bk_s = _t5_bucket(qs_[:, None] - ks_[None, :])
    tb_s = np.ascontiguousarray(t5[bk_s].transpose(2, 0, 1))
    maskS = np.where(ks_[None, :] <= 2111, 0.0, NEG).astype(np.float32) * np.ones((64, 1), np.float32)
    shared.update({"tbAs": np.ascontiguousarray(tb_s[:16]), "tbBs": np.ascontiguousarray(tb_s[16:]), "maskS": np.ascontiguousarray(maskS)})
    drel = f(d_rel_bias)[0]
    rr = np.arange(64)
    maskCs = (np.arange(128)[None, :] < rr[:, None]).astype(np.float32)
    kk = np.arange(640)
    rel_ds = (2048 + rr)[:, None] - (1536 + kk)[None, :]
    tbDs = np.ascontiguousarray(drel[np.clip(rel_ds, -128, 128) + 128].transpose(2, 0, 1))
    maskDs = np.where(kk[None, :] < 576, 0.0, NEG).astype(np.float32) * np.ones((64, 1), np.float32)
    shared.update({"maskCs": np.ascontiguousarray(maskCs), "tbDs": tbDs, "maskDs": np.ascontiguousarray(maskDs)})
    in_maps = []
    for c in range(NCORES):
        xc = np.concatenate([xp[:, c].reshape(1024, D), x_sample[2 * c:2 * c + 2].reshape(128, D)], 0)
        pc = np.concatenate([pp[:, :, c].reshape(2, 1024, 256), p_sample[:, 2 * c:2 * c + 2].reshape(2, 128, 256)], 1)
        m = dict(shared)
        for wn, wv in wfull.items():
            m[wn] = np.ascontiguousarray(wv[c * 512:(c + 1) * 512])
        qp_ = 128 * c + np.arange(128)
        kp_ = np.arange(-128, 1024)
        bk_p = _t5_bucket(qp_[:, None] - kp_[None, :])
        tb_p = np.ascontiguousarray(t5[bk_p].transpose(2, 0, 1))
        maskP = np.where((kp_[None, :] // 64) <= (qp_[:, None] // 64), 0.0, NEG).astype(np.float32)
        kc_ = np.arange(1024)
        maskC = (kc_[None, :] < qp_[:, None]).astype(np.float32)
        kd_ = np.arange(-512, 1024)
        rel_d = qp_[:, None] - kd_[None, :]
        tbD = np.ascontiguousarray(drel[np.clip(rel_d, -128, 128) + 128].transpose(2, 0, 1))
        qc_, kcc_ = qp_[:, None] // 64, kd_[None, :] // 64
        maskD = np.where((kcc_ <= qc_) & (kcc_ >= qc_ - 8), 0.0, NEG).astype(np.float32)
        m.update({"maskC": np.ascontiguousarray(maskC), "tbD": tbD, "maskD": np.ascontiguousarray(maskD)})
        m.update({"tbA": np.ascontiguousarray(tb_p[:16]), "tbB": np.ascontiguousarray(tb_p[16:]), "maskP": np.ascontiguousarray(maskP)})
        m.update({
            "x": np.ascontiguousarray(xc), "p": np.ascontiguousarray(pc),
            "cache_a_kv": f(cache_a_kv)[0, 2 * c:2 * c + 2].reshape(2, 2048, 1024),
            "cache_a_kidx": f(cache_a_kidx)[0, 2 * c:2 * c + 2].reshape(2, 2048, 64),
            "cache_b_kv": f(cache_b_kv)[0, 2 * c:2 * c + 2].reshape(2, 2048, 4096),
            "cache_c_kv": f(cache_c_kv)[0, 2 * c:2 * c + 2].reshape(2, 2048, 4096),
            "cache_d_kv": f(cache_d_kv)[0, 2 * c:2 * c + 2].reshape(2, 512, 4096),
        })
        in_maps.append(m)
    if DEBUG:
        wd = np.ascontiguousarray(wfull["w_in_even"][:, :512])
        in_maps = [dict({k: v for k, v in m.items() if k in USED and k not in BIG}, w_dbg=wd) for m in in_maps]
    else:
        in_maps = [{k: v for k, v in m.items() if k in USED} for m in in_maps]
    res = run_bass_kernel_spmd(nc, in_maps, core_ids=list(range(NCORES)))
    R = res.results

    def gather_tok(key, width):
        full_p = np.zeros((8, 8, 128, width), np.float32)
        full_s = np.zeros((16, 64, width), np.float32)
        for c in range(NCORES):
            a = np.asarray(R[c][key])
            full_p[:, c] = a[:1024].reshape(8, 128, width)
            full_s[2 * c:2 * c + 2] = a[1024:].reshape(2, 64, width)
        return full_p.reshape(8192, width), full_s

    yp, ys = gather_tok("y", D)
    akp, aks = gather_tok("o_akv", 1024)
    aip, ais = gather_tok("o_aki", 64)
    bkp, bks = gather_tok("o_bkv", 4096)
    ckp, cks = gather_tok("o_ckv", 4096)
    dkp, _ = gather_tok("o_dkv", 4096)
    ds = np.concatenate([np.asarray(R[c]["o_ds"]) for c in range(NCORES)], 0)
    return (yp.reshape(1, 8192, D), ys.reshape(16, 64, D),
            akp.reshape(1, 1, 8192, 2, 4, 128), aks.reshape(1, 16, 64, 2, 4, 128),
            aip.reshape(1, 1, 8192, 64), ais.reshape(1, 16, 64, 64),
            bkp.reshape(1, 1, 8192, 2, 8, 256), bks.reshape(1, 16, 64, 2, 8, 256),
            ckp.reshape(1, 1, 8192, 2, 16, 128), cks.reshape(1, 16, 64, 2, 16, 128),
            dkp[8192 - 512:].reshape(1, 1, 512, 2, 16, 128), ds.reshape(1, 16, 512, 2, 16, 128))
```
